# Optimizing a Trainium2 kernel written in Bass

```python
import math
import jax, jax.numpy as jnp
from jax import lax
import numpy as np


D_MODEL = 1024
BATCH = 8
SEQ = 2048
DEPTH = 2

CHUNK = 64
Q_BLOCK = 128
N_EVEN = (DEPTH + 1) // 2
N_ODD = DEPTH // 2

MLA_HEADS = 8
MLA_Q_LORA = 384
MLA_KV_LORA = 256
MLA_NOPE = 64
MLA_ROPE = 32
MLA_V = 64
SB_HEADS = 8
SB_HEAD_DIM = 64
SB_W = SB_HEADS * SB_HEAD_DIM
AB_WIDTH = MLA_HEADS * MLA_V + SB_W
AB_IN = MLA_Q_LORA + MLA_KV_LORA + MLA_ROPE + 3 * SB_W + AB_WIDTH
DIFF_HEADS = 8
DIFF_HEAD_DIM = 64
DIFF_QK = DIFF_HEADS * 2 * DIFF_HEAD_DIM
DIFF_WIDTH = DIFF_HEADS * 2 * DIFF_HEAD_DIM
DIFF_IN = 2 * DIFF_QK + 2 * DIFF_WIDTH

REL_BUCKETS = 32
REL_MAX_DIST = 128
ROPE_THETA = 10000.0
NORM_EPS = 1e-6
SUBLN_EPS = 1e-5
NEG_INF = -1e30

kernel_name = 'hybrid_mla_stickbreak_diffattn_encoder'


def rms_norm(x, g, eps=NORM_EPS):
    xf = x.astype(jnp.float32)
    y = xf * lax.rsqrt(jnp.mean(xf * xf, axis=-1, keepdims=True) + eps)
    return (y * g.astype(jnp.float32)).astype(x.dtype)


def rope(x, pos):
    half = x.shape[-1] // 2
    inv_freq = ROPE_THETA ** (-jnp.arange(half, dtype=jnp.float32) / half)
    ang = pos.astype(jnp.float32)[..., None, None] * inv_freq
    cos, sin = jnp.cos(ang), jnp.sin(ang)
    xf = x.astype(jnp.float32)
    x1, x2 = xf[..., :half], xf[..., half:]
    return jnp.concatenate([x1 * cos - x2 * sin, x1 * sin + x2 * cos], axis=-1).astype(x.dtype)


def t5_bucket(rel):
    nb = REL_BUCKETS // 2
    max_exact = nb // 2
    ret = jnp.where(rel > 0, nb, 0)
    n = jnp.abs(rel)
    nf = jnp.maximum(n, 1).astype(jnp.float32)
    large = max_exact + (jnp.log(nf / max_exact) / math.log(REL_MAX_DIST / max_exact) * (nb - max_exact)).astype(jnp.int32)
    large = jnp.minimum(large, nb - 1)
    return ret + jnp.where(n < max_exact, n, large)


def chunk_mask(qs, ke):
    tq = qs + jnp.arange(Q_BLOCK)
    tk = jnp.arange(ke)
    return (tk // CHUNK)[None, :] <= (tq // CHUNK)[:, None]


def sweep(block_fn, seq):
    return jnp.concatenate([block_fn(i * Q_BLOCK, (i + 1) * Q_BLOCK) for i in range(seq // Q_BLOCK)], axis=1)


def mla_attention(q, k, v):
    scale = (MLA_NOPE + MLA_ROPE) ** -0.5

    def block(qs, ke):
        s = jnp.einsum('bqhd,bkhd->bhqk', q[:, qs:ke], k[:, :ke]).astype(jnp.float32) * scale
        s = jnp.where(chunk_mask(qs, ke), s, NEG_INF)
        p = jax.nn.softmax(s, axis=-1).astype(v.dtype)
        return jnp.einsum('bhqk,bkhd->bqhd', p, v[:, :ke])

    return sweep(block, q.shape[1])


def stick_breaking_attention(q, k, v):
    scale = SB_HEAD_DIM ** -0.5

    def block(qs, ke):
        z = jnp.einsum('bqhd,bkhd->bhqk', q[:, qs:ke], k[:, :ke]).astype(jnp.float32) * scale
        tq = qs + jnp.arange(Q_BLOCK)
        tk = jnp.arange(ke)
        m = tk[None, :] < tq[:, None]
        log_1m = jnp.where(m, jax.nn.log_sigmoid(-z), 0.0)
        csum = jnp.cumsum(log_1m, axis=-1)
        log_w = jax.nn.log_sigmoid(z) + csum[..., -1:] - csum
        w = jnp.where(m, jnp.exp(log_w), 0.0).astype(v.dtype)
        return jnp.einsum('bhqk,bkhd->bqhd', w, v[:, :ke])

    return sweep(block, q.shape[1])


def layer_ab(h, pos, w_in, q_norm_g, kv_norm_g, w_uq, w_ukv, w_out):
    B, S, _ = h.shape
    proj = h @ w_in
    cuts = np.cumsum([MLA_Q_LORA, MLA_KV_LORA, MLA_ROPE, SB_W, SB_W, SB_W]).tolist()
    cq, ckv, kr, q_sb, k_sb, v_sb, z = jnp.split(proj, cuts, axis=-1)
    q = (rms_norm(cq, q_norm_g) @ w_uq).reshape(B, S, MLA_HEADS, MLA_NOPE + MLA_ROPE)
    q = jnp.concatenate([q[..., :MLA_NOPE], rope(q[..., MLA_NOPE:], pos)], axis=-1)
    kv = (rms_norm(ckv, kv_norm_g) @ w_ukv).reshape(B, S, MLA_HEADS, MLA_NOPE + MLA_V)
    k_rope = jnp.broadcast_to(rope(kr[:, :, None, :], pos), (B, S, MLA_HEADS, MLA_ROPE))
    k = jnp.concatenate([kv[..., :MLA_NOPE], k_rope], axis=-1)
    o_mla = mla_attention(q, k, kv[..., MLA_NOPE:]).reshape(B, S, MLA_HEADS * MLA_V)
    shp = (B, S, SB_HEADS, SB_HEAD_DIM)
    o_sb = stick_breaking_attention(q_sb.reshape(shp), k_sb.reshape(shp), v_sb.reshape(shp)).reshape(B, S, SB_W)
    y = jnp.concatenate([o_mla, o_sb], axis=-1) * jax.nn.silu(z)
    return y @ w_out


def layer_diff(h, rel_table, w_in, lq1, lk1, lq2, lk2, subln_g, w_out, lam_init):
    B, S, _ = h.shape
    proj = h @ w_in
    q, k, v, z = jnp.split(proj, [DIFF_QK, 2 * DIFF_QK, 2 * DIFF_QK + DIFF_WIDTH], axis=-1)
    q = q.reshape(B, S, DIFF_HEADS, 2, DIFF_HEAD_DIM)
    k = k.reshape(B, S, DIFF_HEADS, 2, DIFF_HEAD_DIM)
    v = v.reshape(B, S, DIFF_HEADS, 2 * DIFF_HEAD_DIM)
    f32 = jnp.float32
    lam = (jnp.exp(jnp.sum(lq1.astype(f32) * lk1.astype(f32)))
           - jnp.exp(jnp.sum(lq2.astype(f32) * lk2.astype(f32))) + lam_init)
    scale = DIFF_HEAD_DIM ** -0.5
    table = rel_table.astype(f32)

    def block(qs, ke):
        tq = qs + jnp.arange(Q_BLOCK)
        tk = jnp.arange(ke)
        bias = table[t5_bucket(tk[None, :] - tq[:, None])].transpose(2, 0, 1)
        s = jnp.einsum('bqhcd,bkhcd->cbhqk', q[:, qs:ke], k[:, :ke]).astype(f32) * scale + bias
        s = jnp.where(chunk_mask(qs, ke), s, NEG_INF)
        p = jax.nn.softmax(s, axis=-1)
        a = (p[0] - lam * p[1]).astype(v.dtype)
        return jnp.einsum('bhqk,bkhd->bqhd', a, v[:, :ke])

    o = sweep(block, S)
    o = rms_norm(o, subln_g, SUBLN_EPS) * (1.0 - lam_init)
    y = o.reshape(B, S, DIFF_WIDTH) * jax.nn.silu(z)
    return y @ w_out


def setup_inputs(seed: int = 0) -> dict:
    key = jax.random.key(seed)
    ks = jax.random.split(key, 24)
    nrm = lambda k, shape, s: jax.random.normal(k, shape, jnp.float32) * s
    D = D_MODEL
    return {
        'x': nrm(ks[0], (BATCH, SEQ, D), 1.0),
        'c': nrm(ks[1], (BATCH, D), 1.0),
        'pos_offset': (jax.random.randint(ks[2], (BATCH,), 0, 64) * CHUNK).astype(jnp.int32),
        'rel_bias_table': nrm(ks[3], (REL_BUCKETS, DIFF_HEADS), 0.5),
        'ada_w': nrm(ks[4], (DEPTH, D, 3 * D), D ** -0.5),
        'ada_b': nrm(ks[5], (DEPTH, 3 * D), 0.01),
        'norm_g': 1.0 + nrm(ks[6], (DEPTH, D), 0.01),
        'final_g': 1.0 + nrm(ks[7], (D,), 0.01),
        'ab_w_in': nrm(ks[8], (N_EVEN, D, AB_IN), D ** -0.5),
        'ab_q_norm_g': 1.0 + nrm(ks[9], (N_EVEN, MLA_Q_LORA), 0.01),
        'ab_kv_norm_g': 1.0 + nrm(ks[10], (N_EVEN, MLA_KV_LORA), 0.01),
        'ab_w_uq': nrm(ks[11], (N_EVEN, MLA_Q_LORA, MLA_HEADS * (MLA_NOPE + MLA_ROPE)), MLA_Q_LORA ** -0.5),
        'ab_w_ukv': nrm(ks[12], (N_EVEN, MLA_KV_LORA, MLA_HEADS * (MLA_NOPE + MLA_V)), MLA_KV_LORA ** -0.5),
        'ab_w_out': nrm(ks[13], (N_EVEN, AB_WIDTH, D), AB_WIDTH ** -0.5),
        'dif_w_in': nrm(ks[14], (N_ODD, D, DIFF_IN), D ** -0.5),
        'dif_lam_q1': nrm(ks[15], (N_ODD, DIFF_HEAD_DIM), 0.1),
        'dif_lam_k1': nrm(ks[16], (N_ODD, DIFF_HEAD_DIM), 0.1),
        'dif_lam_q2': nrm(ks[17], (N_ODD, DIFF_HEAD_DIM), 0.1),
        'dif_lam_k2': nrm(ks[18], (N_ODD, DIFF_HEAD_DIM), 0.1),
        'dif_subln_g': 1.0 + nrm(ks[19], (N_ODD, 2 * DIFF_HEAD_DIM), 0.01),
        'dif_w_out': nrm(ks[20], (N_ODD, DIFF_WIDTH, D), DIFF_WIDTH ** -0.5),
    }


def reference(x, c, pos_offset, rel_bias_table, ada_w, ada_b, norm_g, final_g,
              ab_w_in, ab_q_norm_g, ab_kv_norm_g, ab_w_uq, ab_w_ukv, ab_w_out,
              dif_w_in, dif_lam_q1, dif_lam_k1, dif_lam_q2, dif_lam_k2, dif_subln_g, dif_w_out):
    B, S, _ = x.shape
    pos = pos_offset[:, None] + jnp.arange(S, dtype=jnp.int32)[None, :]
    c_act = jax.nn.silu(c)
    for i in range(DEPTH):
        mod = c_act @ ada_w[i] + ada_b[i]
        shift, scale, gate = jnp.split(mod, 3, axis=-1)
        h = rms_norm(x, norm_g[i]) * (1.0 + scale[:, None, :]) + shift[:, None, :]
        j = i // 2
        if i % 2 == 0:
            out = layer_ab(h, pos, ab_w_in[j], ab_q_norm_g[j], ab_kv_norm_g[j],
                           ab_w_uq[j], ab_w_ukv[j], ab_w_out[j])
        else:
            lam_init = 0.8 - 0.6 * math.exp(-0.3 * i)
            out = layer_diff(h, rel_bias_table, dif_w_in[j], dif_lam_q1[j], dif_lam_k1[j],
                             dif_lam_q2[j], dif_lam_k2[j], dif_subln_g[j], dif_w_out[j], lam_init)
        x = x + gate[:, None, :] * out
    return rms_norm(x, final_g)
```

```python
import math
import numpy as np
from contextlib import ExitStack
import concourse.bass as bass
import concourse.mybir as mybir
from concourse.bass_utils import run_bass_kernel_spmd

F32 = mybir.dt.float32
BF16 = mybir.dt.bfloat16
I32 = mybir.dt.int32
AF = mybir.ActivationFunctionType
ALU = mybir.AluOpType

_DTS = {F32: 4, BF16: 2, I32: 4}


def _region(ap):
    pat = ap.ap
    ds = _DTS.get(ap.dtype, 4)
    pstride = pat[0][0]
    off = ap.offset
    if pstride == 0:
        p0, f0, np_ = 0, off, 1
    else:
        p0, f0, np_ = off // pstride, off % pstride, pat[0][1]
    lo = hi = f0
    for st, cnt in pat[1:]:
        if st >= 0:
            hi += st * (cnt - 1)
        else:
            lo += st * (cnt - 1)
    if ap.tensor.name.startswith("PB"):
        return (ap.tensor.name, 0, 128, 0, 2048)
    return (ap.tensor.name, p0, p0 + np_, lo * ds, (hi + 1) * ds)


class Op:
    __slots__ = ("eng", "fn", "deps", "idx", "sig", "signum", "dma", "dsem", "dval")


class Sched:
    ENG = ("pe", "act", "dve", "pool", "sp")

    def __init__(self, nc):
        self.nc = nc
        self.ops = []
        self.rec = {}
        self.dkeys = {}

    def _deps_for(self, idx, eng, is_dma, reads, writes):
        deps = set()
        for ap in reads:
            name, p0, p1, b0, b1 = _region(ap)
            lst = self.rec.setdefault(name, [])
            psum = name.startswith("PB")
            for r in lst:
                if r[5] and r[0] < p1 and p0 < r[1] and r[2] < b1 and b0 < r[3]:
                    deps.add(r[4])
                elif psum and not r[5] and self.ops[r[4]].eng != eng:
                    deps.add(r[4])
            key = (p0, p1, b0, b1)
            for r in lst:
                if (not r[5]) and (r[0], r[1], r[2], r[3]) == key:
                    o = self.ops[r[4]]
                    if o.eng == eng and not o.dma and not is_dma:
                        r[4] = idx
                        break
            else:
                lst.append([p0, p1, b0, b1, idx, False])
        for ap in writes:
            name, p0, p1, b0, b1 = _region(ap)
            lst = self.rec.setdefault(name, [])
            keep = []
            for r in lst:
                if r[0] < p1 and p0 < r[1] and r[2] < b1 and b0 < r[3]:
                    if r[4] != idx:
                        deps.add(r[4])
                    if r[0] >= p0 and r[1] <= p1 and r[2] >= b0 and r[3] <= b1 and r[4] != idx:
                        continue
                keep.append(r)
            keep.append([p0, p1, b0, b1, idx, True])
            self.rec[name] = keep
        deps.discard(idx)
        return deps

    def op(self, eng, fn, outs, ins):
        o = Op()
        o.eng, o.fn, o.idx, o.dma, o.sig, o.signum = eng, fn, len(self.ops), False, False, 0
        o.dsem = o.dval = None
        self.ops.append(o)
        o.deps = self._deps_for(o.idx, eng, False, ins, outs)
        return o

    def dma(self, queue, out, in_, key, sb_out=True, sb_in=False):
        o = Op()
        o.eng, o.idx, o.dma, o.sig, o.signum = queue, len(self.ops), True, False, 0
        o.fn = lambda e: e.dma_start(out=out, in_=in_)
        self.ops.append(o)
        o.deps = self._deps_for(o.idx, queue, True, [in_] if sb_in else [], [out] if sb_out else [])
        st = self.dkeys.setdefault(key, [None, 0])
        if st[0] is not None:
            o.deps.add(st[0])
        st[0] = o.idx
        st[1] += 16
        o.dsem, o.dval = key, st[1]
        return o

    def emit(self, stack):
        nc = self.nc
        engs = {"pe": nc.tensor, "act": nc.scalar, "dve": nc.vector, "pool": nc.gpsimd, "sp": nc.sync}
        ops = self.ops
        known = {e: {} for e in self.ENG}
        vcs = [None] * len(ops)
        plan = []
        self.nwaits = 0

        def merge(kn, vc):
            for k, v in vc.items():
                if kn.get(k, -1) < v:
                    kn[k] = v
        for o in ops:
            waits_e = {}
            waits_d = {}
            for d in o.deps:
                p = ops[d]
                if p.dma:
                    if waits_d.get(p.dsem, (0, -1))[0] < p.dval:
                        waits_d[p.dsem] = (p.dval, d)
                else:
                    if p.eng == o.eng and not o.dma and o.eng == "pe":
                        continue
                    if waits_e.get(p.eng, -1) < d:
                        waits_e[p.eng] = d
            kn = known[o.eng]
            we = {}
            for e, d in sorted(waits_e.items(), key=lambda x: -x[1]):
                if kn.get(("e", e), -1) >= d:
                    continue
                kn[("e", e)] = d
                merge(kn, vcs[d])
                ops[d].sig = True
                we[e] = d
            wd = {}
            for k, (v, d) in waits_d.items():
                if kn.get(("d", k), 0) >= v:
                    continue
                kn[("d", k)] = v
                merge(kn, vcs[d])
                wd[k] = v
            self.nwaits += len(we) + len(wd)
            vc = dict(kn)
            if o.dma:
                vc[("d", o.dsem)] = o.dval
            else:
                vc[("e", o.eng)] = max(vc.get(("e", o.eng), -1), o.idx)
            vcs[o.idx] = vc
            plan.append((we, wd))
        cnt = {e: 0 for e in self.ENG}
        for o in ops:
            if o.sig and not o.dma:
                cnt[o.eng] += 1
                o.signum = cnt[o.eng]
        esem = {e: stack.enter_context(nc.semaphore("s_" + e)) for e in self.ENG}
        dsem = {k: stack.enter_context(nc.semaphore("d_" + str(k))) for k in self.dkeys}
        for o, (we, wd) in zip(ops, plan):
            eh = engs[o.eng]
            for e, d in we.items():
                eh.wait_ge(esem[e], ops[d].signum)
            for k, v in wd.items():
                eh.wait_ge(dsem[k], v)
            ins = o.fn(eh)
            if o.dma:
                ins.then_inc(dsem[o.dsem], 16)
            elif o.sig:
                ins.then_inc(esem[o.eng], 1)
        for k, st in self.dkeys.items():
            nc.sync.wait_ge(dsem[k], st[1])
        return cnt


D = 1024
S_ = 2048
NT = 16
NEG = -30000.0


def _t5_bucket(rel):
    nb, me = 16, 8
    ret = np.where(rel > 0, nb, 0)
    n = np.abs(rel)
    nf = np.maximum(n, 1).astype(np.float32)
    large = me + (np.log(nf / me) / math.log(128 / me) * (nb - me)).astype(np.int32)
    large = np.minimum(large, nb - 1)
    return ret + np.where(n < me, n, large)


def _consts():
    c = {}
    j = np.arange(128)[:, None]
    t = np.arange(128)[None, :]
    m = np.zeros((128, 8, 128), np.float32)
    m[:, 0] = np.eye(128)
    m[:, 1] = (j == 127 - t)
    m[:, 2] = (j >= t)
    m[:, 3] = 1.0
    m[:, 4] = 0.0
    m[:, 5] = np.where(j < t, 0.0, NEG)
    m[:, 6] = np.where((j // 64) <= (t // 64), 0.0, NEG)
    c["cmat"] = m.reshape(128, 1024)
    cc = np.zeros((128, 8), np.float32)
    p = np.arange(128)
    cc[:, 0] = 10000.0 ** (-(p % 16) / 16.0)
    cc[:, 1] = np.where((p % 32) < 16, -1.0, 1.0)
    c["ccol"] = cc
    oh = np.zeros((32, 384), np.float32)
    i = np.arange(383)
    b = _t5_bucket(127 - i)
    oh[b, i] = 1.0
    oh[15, :383] -= 1.0
    oh *= 8.0
    c["oh1d"] = oh
    sel = np.zeros((32, 128), np.float32)
    sel[15, :] = 1.0
    c["sel15"] = sel
    c["trow"] = np.arange(S_, dtype=np.float32)[None, :].repeat(128, 0)
    return c


def build(layers, final):
    nc = bass.Bass("TRN2", target_bir_lowering=False)
    dt = lambda n, s, d=F32, k="ExternalInput": nc.dram_tensor(n, s, d, kind=k).ap()
    x_d = dt("x", [S_, D])
    out_d = dt("out", [S_, D], F32, "ExternalOutput")
    ct_d = dt("c_t", [128, 8])
    pos_d = dt("posi", [128, 1], I32)
    adaw_d = dt("ada_w", [2, D, 3 * D])
    adab_d = dt("ada_b_t", [128, 2, 24])
    ng_d = dt("norm_g_t", [128, 2, 8])
    fg_d = dt("final_g_bc", [128, D])
    win0_d = dt("ab_w_in", [D, 3232])
    gq_d = dt("gq_t", [128, 3])
    gkv_d = dt("gkv_t", [128, 2])
    wuq_d = dt("ab_w_uq", [384, 768])
    wukv_d = dt("ab_w_ukv", [256, 1024])
    wout0_d = dt("ab_w_out", [D, D])
    win1_d = dt("dif_w_in", [D, 4096])
    lamv_d = dt("lamv", [128, 4, 64])
    sg_d = dt("subln_t", [128, 1])
    wout1_d = dt("dif_w_out", [D, D])
    tab_d = dt("rel_tab", [32, 8])
    cmat_d = dt("cmat", [128, 1024])
    ccol_d = dt("ccol", [128, 8])
    oh_d = dt("oh1d", [32, 384])
    sel_d = dt("sel15", [32, 128])
    trow_d = dt("trow", [128, S_])
    gscr = nc.dram_tensor("gscr", [8, 384], F32, kind="Internal").ap()

    st = ExitStack()
    with st:
        sbt = lambda n, s, d: st.enter_context(nc.sbuf_tensor(n, s, d))
        X = sbt("X", [128, NT, D], F32)
        NSL = 27
        SL = sbt("SL", [128, NSL, 2048], BF16)
        MW = sbt("MW", [128, 5376], BF16)
        GBC = sbt("GBC", [128, D], F32)
        NSC = 7
        SC = sbt("SC", [128, NSC, 512], F32)
        CB = sbt("CB", [128, 8, 128], BF16)
        CF = sbt("CF", [128, 2, 128], F32)
        CC = sbt("CC", [128, 8], F32)
        SM = sbt("SM", [128, 192], F32)
        SMB = sbt("SMB", [128, 8], BF16)
        TB = sbt("TB", [32, 8 + 384 + 128], F32)
        POSI = sbt("POSI", [128, 1], I32)
        LAMV = sbt("LAMV", [128, 4, 64], F32)
        PB = [st.enter_context(nc.psum_tensor("PB%d" % i, [128, 512], F32)) for i in range(8)]

        S = Sched(nc)
        sl = lambda i: SL[:, i, :]
        ident_b, anti_b, U_b, ones_b, zero_b, negtri_b, negchk_b = [CB[:, i, :] for i in range(7)]
        ident_f, ones_f = CF[:, 0, :], CF[:, 1, :]

        def ACT(out, in_, func, scale=1.0, bias=0.0, accum=None, extra_in=()):
            kw = {}
            if accum is not None:
                kw["accum_out"] = accum
            ins = [in_] + list(extra_in)
            if not isinstance(scale, (int, float)):
                ins.append(scale)
            if not isinstance(bias, (int, float)):
                ins.append(bias)
            outs = [out] + ([accum] if accum is not None else [])
            if not (isinstance(scale, (int, float)) and scale == 1.0):
                kw["scale"] = scale
            if not (isinstance(bias, (int, float)) and bias == 0.0):
                kw["bias"] = bias
            return S.op("act", lambda e: e.activation(out=out, in_=in_, func=func, **kw), outs, ins)

        def TT(eng, out, in0, in1, op):
            return S.op(eng, lambda e: e.tensor_tensor(out=out, in0=in0, in1=in1, op=op), [out], [in0, in1])

        def TS(eng, out, in0, s1, s2, op0, op1=None):
            ins = [in0] + [s for s in (s1, s2) if s is not None and not isinstance(s, (int, float))]
            if op1 is None:
                return S.op(eng, lambda e: e.tensor_scalar(out=out, in0=in0, scalar1=s1, scalar2=None, op0=op0), [out], ins)
            return S.op(eng, lambda e: e.tensor_scalar(out=out, in0=in0, scalar1=s1, scalar2=s2, op0=op0, op1=op1), [out], ins)

        def STT(eng, out, in0, sc, in1, op0, op1):
            ins = [in0, in1] + ([sc] if not isinstance(sc, (int, float)) else [])
            return S.op(eng, lambda e: e.scalar_tensor_tensor(out=out, in0=in0, scalar=sc, in1=in1, op0=op0, op1=op1), [out], ins)

        def CP(eng, out, in_):
            return S.op(eng, lambda e: e.tensor_copy(out=out, in_=in_), [out], [in_])

        def MSET(eng, out, v):
            return S.op(eng, lambda e: e.memset(out, v), [out], [])

        def RECIP(out, in_):
            return S.op("dve", lambda e: e.reciprocal(out=out, in_=in_), [out], [in_])

        def MM(outreg, mms):
            ins = []
            for (_, l, r, _, _) in mms:
                ins += [l, r]

            def fn(e):
                last = None
                for (o, l, r, a, b) in mms:
                    last = e.matmul(o, lhsT=l, rhs=r, start=a, stop=b)
                return last
            return S.op("pe", fn, [outreg], ins)

        def TR(outreg, trs):
            ins = []
            for (_, i, idn) in trs:
                ins += [i, idn]

            def fn(e):
                last = None
                for (o, i, idn) in trs:
                    last = e.transpose(o, i, idn)
                return last
            return S.op("pe", fn, [outreg], ins)

        dq = [0]

        def wdma(out, in_, key):
            return S.dma("pool", out, in_, key)

        def sc(i, w=512):
            return SC[:, i, 0:w]

        def sc2(i):
            return SC[:, i:i + 2, :].rearrange("p a b -> p (a b)")

        CST = SC[:, 0:2, :].rearrange("p a b -> p (a b)")
        S.dma("sp", CST, cmat_d, "c0")
        S.dma("sp", CC[:], ccol_d, "c1")
        S.dma("sp", TB[:, 0:8], tab_d, "c2")
        S.dma("sp", TB[:, 8:392], oh_d, "c2")
        S.dma("sp", TB[:, 392:520], sel_d, "c2")
        S.dma("sp", POSI[:], pos_d, "c1")
        S.dma("sp", LAMV[:], lamv_d, "c1")
        S.dma("sp", SM[:, 0:8], ct_d, "c3")
        S.dma("sp", SM[:, 8:56].rearrange("p (a b) -> p a b", b=24), adab_d, "c3")
        S.dma("sp", SM[:, 56:72].rearrange("p (a b) -> p a b", b=8), ng_d, "c3")
        S.dma("sp", SM[:, 72:75], gq_d, "c3")
        S.dma("sp", SM[:, 75:77], gkv_d, "c3")
        S.dma("sp", SM[:, 77:78], sg_d, "c3")
        CT, ADAB, NG, GQ, GKV, SG = SM[:, 0:8], SM[:, 8:56], SM[:, 56:72], SM[:, 72:75], SM[:, 75:77], SM[:, 77:78]
        MODC = SM[:, 80:128]
        GMOD = SM[:, 128:144]
        POSF = SM[:, 144:145]
        NLAM = SM[:, 145:146]
        LTMP = SM[:, 146:150]
        CBIAS = SM[:, 150:158]
        CP("dve", CB[:].rearrange("p a b -> p (a b)"), CST)
        CP("dve", CF[:, 0, :], CST[:, 0:128])
        CP("dve", CF[:, 1, :], CST[:, 384:512])
        if 0 in layers:
            COS, SIN = sl(15), sl(16)
            MSET("pool", SM[:, 159:160], math.pi / 2)
            CP("dve", POSF, POSI[:])
            for nb in range(4):
                ang = sc(0)
                S.dma("sp", ang, trow_d[:, nb * 512:(nb + 1) * 512], "trow")
                TS("dve", ang, ang, POSF, CC[:, 0:1], ALU.add, ALU.mult)
                kf, ki = sc(1), sc(3).bitcast(I32)
                TS("dve", kf, ang, 1.0 / (2 * math.pi), None, ALU.mult)
                CP("dve", ki, kf)
                CP("dve", kf, ki)
                STT("dve", ang, kf, -2.0 * math.pi, ang, ALU.mult, ALU.add)
                s2, c2 = sc(1), sc(2)
                ACT(s2, ang, AF.Sin, scale=0.5)
                ACT(c2, ang, AF.Sin, scale=-0.5, bias=SM[:, 159:160])
                STT("dve", c2, s2, 2.0, c2, ALU.mult, ALU.mult)
                TS("dve", SIN[:, nb * 512:(nb + 1) * 512], c2, CC[:, 1:2], None, ALU.mult)
                TT("dve", s2, s2, s2, ALU.mult)
                TS("dve", COS[:, nb * 512:(nb + 1) * 512], s2, -2.0, 1.0, ALU.mult, ALU.add)

        for tt in range(NT):
            S.dma("sp", X[:, tt, :], x_d[tt * 128:(tt + 1) * 128, :], "x%d" % (tt % 4))

        ACT(SMB[:, 0:8], CT, AF.Silu)
        CACT = SMB[:, 0:8]
        adv = adaw_d.rearrange("l (k p) n -> l p k n", p=128)
        DEFER = DEFER_ADA and (list(layers) == [0, 1])

        def mod_dma(l, nb, wb, key):
            wdma(wb, adv[l, :, :, nb * 512:(nb + 1) * 512], key)

        def mod_mm(l, nb, wb):
            mms = []
            for jj in range(4):
                col = PB[6][:, nb * 4 + jj:nb * 4 + jj + 1]
                for k in range(8):
                    mms.append((col, wb[:, k, jj * 128:(jj + 1) * 128], CACT[:, k:k + 1], k == 0, k == 7))
            MM(PB[6][:, 0:32], mms)
            o_ = MODC[:, l * 24 + nb * 4:l * 24 + nb * 4 + 4]
            i0_, i1_ = PB[6][:, nb * 4:nb * 4 + 4], ADAB[:, l * 24 + nb * 4:l * 24 + nb * 4 + 4]
            S.op("dve", lambda e: e.tensor_tensor(out=o_, in0=i0_, in1=i1_, op=ALU.add), [o_], [PB[6][:, 0:32], i1_])

        def mod_fin(l):
            TS("dve", GMOD[:, l * 8:(l + 1) * 8], MODC[:, l * 24 + 8:l * 24 + 16], 1.0, None, ALU.add)
            TT("dve", GMOD[:, l * 8:(l + 1) * 8], GMOD[:, l * 8:(l + 1) * 8], NG[:, l * 8:(l + 1) * 8], ALU.mult)

        def stream_buf(bi):
            return SL[:, 23 + 2 * bi:25 + 2 * bi, :].rearrange("p a b -> p (a b)").rearrange("p (k n) -> p k n", n=512)

        blk_i = [0]
        for li, l in enumerate(layers[:1] if DEFER else layers):
            if li == 0:
                pb4 = [SL[:, 2 * q:2 * q + 2, :].rearrange("p a b -> p (a b)").rearrange("p (k n) -> p k n", n=512) for q in range(4)]
                for nb in range(4):
                    mod_dma(l, nb, pb4[nb], "adp%d" % nb)
                for nb in (4, 5):
                    mod_dma(l, nb, stream_buf(nb % 2), "wst%d" % (nb % 2))
                for nb in range(6):
                    mod_mm(l, nb, pb4[nb] if nb < 4 else stream_buf(nb % 2))
            else:
                for nb in range(6):
                    bi = blk_i[0] % 2
                    blk_i[0] += 1
                    mod_dma(l, nb, stream_buf(bi), "wst%d" % bi)
                    mod_mm(l, nb, stream_buf(bi))
            mod_fin(l)


        def norm_phase(l, hook=None):
            SS = SM[:, 158:159]
            RSTD = SM[:, 160:176]
            SSQ = SM[:, 176:192]
            MSET("pool", SSQ, 0.0)
            for tt in range(NT):
                ACT(sc2(0), X[:, tt, :], AF.Square, accum=SSQ[:, tt:tt + 1])
            TS("dve", SSQ, SSQ, 1.0 / D, 1e-6, ALU.mult, ALU.add)
            ACT(SSQ, SSQ, AF.Ln)
            ACT(RSTD, SSQ, AF.Exp, scale=-0.5)
            def xs_of(tt):
                return sc2(2 * (tt % 2))

            ACT(xs_of(0), X[:, 0, :], AF.Identity, scale=RSTD[:, 0:1])
            for tt in range(NT):
                xs = xs_of(tt)
                if tt + 1 < NT:
                    ACT(xs_of(tt + 1), X[:, tt + 1, :], AF.Identity, scale=RSTD[:, tt + 1:tt + 2])
                for half in range(2):
                    bank = PB[(tt * 2 + half) % 4]
                    TR(bank[:, :], [(bank[:, c * 128:(c + 1) * 128], xs[:, (half * 4 + c) * 128:(half * 4 + c + 1) * 128], ident_f) for c in range(4)])
                    for c in range(4):
                        cc = half * 4 + c
                        if half == 0:
                            TS("dve", SL[:, cc, tt * 128:(tt + 1) * 128], bank[:, c * 128:(c + 1) * 128],
                               GMOD[:, l * 8 + cc:l * 8 + cc + 1], MODC[:, l * 24 + cc:l * 24 + cc + 1], ALU.mult, ALU.add)
                        else:
                            ACT(SL[:, cc, tt * 128:(tt + 1) * 128], bank[:, c * 128:(c + 1) * 128], AF.Identity,
                                scale=GMOD[:, l * 8 + cc:l * 8 + cc + 1], bias=MODC[:, l * 24 + cc:l * 24 + cc + 1])
                if hook is not None and tt % 4 == 3:
                    hook(tt // 4)

        def gate_bc(l):
            for c in range(8):
                dg = SC[:, 6, 0:128]
                TS("dve", dg, ident_f, MODC[:, l * 24 + 16 + c:l * 24 + 17 + c], None, ALU.mult)
                MM(PB[7][:, (c % 4) * 128:(c % 4 + 1) * 128], [(PB[7][:, (c % 4) * 128:(c % 4 + 1) * 128], ones_f, dg, True, True)])
                CP("dve", GBC[:, c * 128:(c + 1) * 128], PB[7][:, (c % 4) * 128:(c % 4 + 1) * 128])

        _rot = [0]
        ROT = [6, 7, 0, 1, 4, 5]

        def nbank():
            b = PB[ROT[_rot[0] % len(ROT)]]
            _rot[0] += 1
            return b

        def out_proj_load(wout_d, g):
            wo = SL[:, 10, :].rearrange("p (c n) -> p c n", n=1024)
            wv = wout_d.rearrange("(c p) n -> p c n", p=128)
            wdma(wo, wv[:, 2 * g:2 * g + 2, :], "wout")
            for c in range(2):
                TT("dve", wo[:, c, :], wo[:, c, :], GBC[:], ALU.mult)

        def out_proj(wout_d, g, ychunks):
            wo = SL[:, 10, :].rearrange("p (c n) -> p c n", n=1024)
            for tt in range(NT):
                for nb in range(2):
                    bank = PB[[6, 7, 0, 1][(tt * 2 + nb) % 4]]
                    MM(bank[:, :], [(bank[:, :], sl(ychunks[c])[:, tt * 128:(tt + 1) * 128], wo[:, c, nb * 512:(nb + 1) * 512], c == 0, c == 1) for c in range(2)])
                    xv = X[:, tt, nb * 512:(nb + 1) * 512]
                    ti = tt * 2 + nb
                    if ti % 3 == 2:
                        tmp = sc(ti // 3 % 2)
                        ACT(tmp, bank[:, :], AF.Copy)
                        TT("pool", xv, xv, tmp, ALU.add)
                    else:
                        TT("dve", xv, xv, bank[:, :], ALU.add)

        def proj_fm2(dA, dB, w, cols):
            for nb in range(4):
                bank = nbank()
                MM(bank[:, :], [(bank[:, :], w[:, k, cols[0]:cols[1]], SL[:, k, nb * 512:(nb + 1) * 512], k == 0, k == 7) for k in range(8)])
                ACT(dA[0:64, nb * 512:(nb + 1) * 512], bank[0:64, :], AF.Copy)
                ACT(dB[64:128, nb * 512:(nb + 1) * 512], bank[64:128, :], AF.Copy)

        def load4(wview, cols4, bi):
            wb = stream_buf(bi)
            for i4, c0 in enumerate(cols4):
                wdma(wb[:, :, i4 * 128:(i4 + 1) * 128], wview[:, :, c0:c0 + 128], "wst%d" % bi)
            return wb

        def proj_fm(dst, w, cols, nrows, func=AF.Copy, row0=0):
            for nb in range(4):
                bank = nbank()
                MM(bank[row0:row0 + nrows, :], [(bank[row0:row0 + nrows, :], w[:, k, cols[0]:cols[1]], SL[:, k, nb * 512:(nb + 1) * 512], k == 0, k == 7) for k in range(8)])
                ACT(dst[row0:row0 + nrows, nb * 512:(nb + 1) * 512], bank[row0:row0 + nrows, :], func)

        def wblock(wview, c0, ncols, bi):
            wb = SL[:, 23 + 2 * bi:25 + 2 * bi, :].rearrange("p a b -> p (a b)").rearrange("p (k n) -> p k n", n=512)
            wdma(wb[:, :, 0:ncols], wview[:, :, c0:c0 + ncols], "wst%d" % bi)
            return wb

        _pl = {"t": 0, "due": {}}

        def defer(delay, fn):
            _pl["due"].setdefault(_pl["t"] + delay, []).append(fn)

        def run_pipeline(items, stages):
            n = len(items)
            ml = max(l for _, l in stages)
            _pl["due"] = {}
            for t in range(n + ml):
                _pl["t"] = t
                for fn in _pl["due"].pop(t, []):
                    fn()
                for fn, lag in stages:
                    ii = t - lag
                    if 0 <= ii < n:
                        fn(ii, items[ii])
            t = n + ml
            while _pl["due"]:
                _pl["t"] = t
                for fn in _pl["due"].pop(t, []):
                    fn()
                t += 1

        ZB4 = [PB[0], PB[1], PB[4], PB[5]]

        ZB3 = [PB[0], PB[1], PB[7]]

        def sm_A(idx, qT, kT, rows, kt, q0, scale, bias, extra, zbs=None):
            d = kt - q0 // 128
            c0 = max(d, 0) * 128
            n = 512 - c0
            zbs = zbs or ZB4
            zb = zbs[idx % len(zbs)]
            mms = [(zb[:, c0:512], kT[rows[0]:rows[1], kt * 128:(kt + 1) * 128], qT[rows[0]:rows[1], q0 + c0:q0 + 512], True, len(extra) == 0)]
            for i, (cs, l, r) in enumerate(extra):
                mms.append((zb[:, cs:cs + 128], l, r, False, i == len(extra) - 1))
            MM(zb[:, c0:512], mms)
            P = SL[:, 22, (idx % 4) * 512:(idx % 4 + 1) * 512]
            ACT(P[:, 0:n], zb[:, c0:512], AF.Exp, scale=scale, bias=bias)
            return P, c0, n

        def RECIPA(out, in_, scratch):
            return S.op("dve", lambda e: e.reciprocal_approx_accurate(out=out, in_=in_, scratch=scratch), [out, scratch], [in_])

        def sm_B(P, c0, n, obank, dbank, v_lhsT, last, first=False):
            assert not first or (c0 == 0 and n == 512)
            MM(obank[:, c0:512], [(obank[:, c0:512], v_lhsT, P[:, 0:n], first, last)])
            if dbank is not None:
                MM(dbank[:, c0:512], [(dbank[:, c0:512], ones_b, P[:, 0:n], first, last)])

        def zinit(bank):
            MM(bank[:, :], [(bank[:, :], zero_b, SL[:, 0, 0:512], True, False)])

        HK = MW[:, 0:4096].rearrange("p (h a t) -> p h a t", a=2, t=128)
        SGP = SM[:, 78:79]
        prep_done = [False]

        def diff_prep(lam_init):
            if prep_done[0]:
                return
            prep_done[0] = True
            junk = SC[:, 6, 0:64]
            for i in range(2):
                S.op("dve", (lambda a, b, c: (lambda e: e.tensor_tensor(out=junk, in0=a, in1=b, op=ALU.mult)))(LAMV[:, 2 * i, :], LAMV[:, 2 * i + 1, :], None), [junk], [LAMV[:, 2 * i, :], LAMV[:, 2 * i + 1, :]])
                S.op("dve", (lambda o: (lambda e: e.reduce_sum(out=o, in_=junk, axis=mybir.AxisListType.X)))(LTMP[:, i:i + 1]), [LTMP[:, i:i + 1]], [junk])
            ACT(LTMP[:, 0:2], LTMP[:, 0:2], AF.Exp)
            TT("dve", NLAM, LTMP[:, 1:2], LTMP[:, 0:1], ALU.subtract)
            TS("dve", NLAM, NLAM, -lam_init, None, ALU.add)
            TS("dve", SGP, SG, 1.0 - lam_init, None, ALU.mult)
            MM(PB[6][:, 0:8], [(PB[6][:, 0:8], TB[:, 392:520], TB[:, 0:8], True, True)])
            CP("dve", CBIAS, PB[6][:, 0:8])
            MM(PB[7][0:8, 0:384], [(PB[7][0:8, 0:384], TB[:, 0:8], TB[:, 8:392], True, True)])
            gv = SC[0:8, 5, 0:384]
            CP("dve", gv, PB[7][0:8, 0:384])
            S.dma("sp", gscr, gv, "gscr", sb_out=False, sb_in=True)
            for h in range(8):
                hk = SC[:, 4, 0:256].rearrange("p (a t) -> p a t", t=128)
                for a in range(2):
                    src = bass.AP(gscr.tensor, h * 384 + a * 128, [[1, 128], [1, 128]])
                    o = S.dma("sp", hk[:, a, :], src, "hk")
                    o.deps.add(S.dkeys["gscr"][0])
                CP("dve", HK[:, h], hk)

        def layer_ab(l):
            w0v = win0_d.rearrange("(k p) n -> p k n", p=128)
            WUQ = MW[:, 0:2304].rearrange("p (k n) -> p k n", n=768)
            WUQR = MW[:, 2304:3072].rearrange("p (k n) -> p k n", n=256)
            WUKV = MW[:, 3072:5120].rearrange("p (k n) -> p k n", n=1024)
            WKR = MW[:, 5120:5376].rearrange("p (k n) -> p k n", n=32)
            wdma(WUQ, wuq_d.rearrange("(k p) n -> p k n", p=128), "wuq")
            wdma(WUKV, wukv_d.rearrange("(k p) n -> p k n", p=128), "wukv")
            wq4 = WUQ.rearrange("p k (h c) -> p k h c", c=96)
            wr4 = WUQR.rearrange("p k (h c) -> p k h c", c=32)
            for k in range(3):
                CP("pool", wr4[:, k, :, 0:16], wq4[:, k, :, 80:96])
                CP("pool", wr4[:, k, :, 16:32], wq4[:, k, :, 64:80])
            WG0 = SL[:, 23:27, :].rearrange("p a b -> p (a b)")[:, 0:8 * 672].rearrange("p (k n) -> p k n", n=672)
            wdma(WG0[:, :, 0:336], w0v[:, :, 0:336], "wst0")
            wdma(WG0[:, :, 336:672], w0v[:, :, 336:672], "wst1")
            for k in range(8):
                CP("pool", WKR[:, k, 0:16], WG0[:, k, 656:672])
                CP("pool", WKR[:, k, 16:32], WG0[:, k, 640:656])
            COS, SIN = sl(15), sl(16)
            CQ = [sl(8 + 3 + i) for i in range(3)]
            CKV = [sl(14), sl(17)]
            KT = sl(18)
            QT = sl(19)
            VA = sl(20).rearrange("p (t c) -> p t c", c=128)
            SZ = sl(21)

            lrot = [0]

            def latent_nb(dsts, col0, gcols, nfeat, nb, ssb):
                nch = len(dsts)
                for c in range(nch):
                    bank = PB[[6, 7, 4][lrot[0] % 3]]
                    lrot[0] += 1
                    MM(bank[:, :], [(bank[:, :], WG0[:, k, col0 + c * 128:col0 + (c + 1) * 128], SL[:, k, nb * 512:(nb + 1) * 512], k == 0, k == 7) for k in range(8)])
                    ACT(dsts[c][:, nb * 512:(nb + 1) * 512], bank[:, :], AF.Copy)
                    sq = SL[:, 22, (c % 2) * 512:(c % 2 + 1) * 512]
                    ACT(sq, bank[:, :], AF.Square)
                    MM(ssb[:, :], [(ssb[:, :], ones_b, sq, c == 0, c == nch - 1)])
                r = sc(4 + nb % 2)
                TS("dve", r, ssb[:, :], 1.0 / nfeat, 1e-6, ALU.mult, ALU.add)
                ACT(r, r, AF.Ln)
                ACT(r, r, AF.Exp, scale=-0.5)
                for c in range(nch):
                    STT("dve", dsts[c][:, nb * 512:(nb + 1) * 512], dsts[c][:, nb * 512:(nb + 1) * 512], gcols[:, c:c + 1], r, ALU.mult, ALU.mult)

            def g0_hook(nb):
                latent_nb(CQ, 0, GQ, 384, nb, PB[5])
                latent_nb(CKV, 384, GKV, 256, nb, PB[5])
            norm_phase(l, g0_hook)
            gate_bc(l)
            for nb in range(4):
                ba, bb = nbank(), nbank()
                MM(ba[0:96, :], [(ba[0:96, :], WG0[:, k, 576:672], SL[:, k, nb * 512:(nb + 1) * 512], k == 0, k == 7) for k in range(8)])
                MM(bb[64:96, :], [(bb[64:96, :], WKR[:, k, :], SL[:, k, nb * 512:(nb + 1) * 512], k == 0, k == 7) for k in range(8)])
                t1, t2 = SC[64:96, 2, :], SC[64:96, 3, :]
                TT("dve", t1, ba[64:96, :], COS[64:96, nb * 512:(nb + 1) * 512], ALU.mult)
                TT("dve", t2, bb[64:96, :], SIN[64:96, nb * 512:(nb + 1) * 512], ALU.mult)
                TT("pool", KT[64:96, nb * 512:(nb + 1) * 512], t1, t2, ALU.add)
            sc_mla = 96.0 ** -0.5
            wz_next = [wblock(w0v, 2208, 128, 0)]
            for h in range(8):
                par = h % 2
                ro = (0, 64) if par == 0 else (64, 128)
                rd = (64, 128) if par == 0 else (0, 64)
                if h % 4 == 3:
                    out_proj_load(wout0_d, h // 4)
                if par == 0:
                    wz = wz_next[0]
                    proj_fm(SZ, wz, (0, 128), 128, AF.Silu)
                    if h < 6:
                        wz_next[0] = wblock(w0v, 2208 + (h // 2 + 1) * 128, 128, (h // 2 + 1) % 2)
                for nb in range(4):
                    ba, bb = nbank(), nbank()
                    MM(ba[0:96, :], [(ba[0:96, :], WUQ[:, k, h * 96:(h + 1) * 96], CQ[k][:, nb * 512:(nb + 1) * 512], k == 0, k == 2) for k in range(3)])
                    MM(bb[64:96, :], [(bb[64:96, :], WUQR[:, k, h * 32:(h + 1) * 32], CQ[k][:, nb * 512:(nb + 1) * 512], k == 0, k == 2) for k in range(3)])
                    ACT(QT[0:64, nb * 512:(nb + 1) * 512], ba[0:64, :], AF.Copy)
                    t1, t2 = SC[64:96, 2, :], SC[64:96, 3, :]
                    TT("dve", t1, ba[64:96, :], COS[64:96, nb * 512:(nb + 1) * 512], ALU.mult)
                    TT("dve", t2, bb[64:96, :], SIN[64:96, nb * 512:(nb + 1) * 512], ALU.mult)
                    TT("pool", QT[64:96, nb * 512:(nb + 1) * 512], t1, t2, ALU.add)
                for nb in range(4):
                    bank = nbank()
                    MM(bank[0:64, :], [(bank[0:64, :], WUKV[:, k, h * 128:h * 128 + 64], CKV[k][:, nb * 512:(nb + 1) * 512], k == 0, k == 1) for k in range(2)])
                    ACT(KT[0:64, nb * 512:(nb + 1) * 512], bank[0:64, :], AF.Copy)
                MSET("pool", VA[:, :, rd[0]:rd[1]], 1.0)
                for half in range(2):
                    bank = nbank()
                    mms = []
                    for t8 in range(8):
                        tt = half * 8 + t8
                        for k in range(2):
                            mms.append((bank[:, t8 * 64:(t8 + 1) * 64], CKV[k][:, tt * 128:(tt + 1) * 128], WUKV[:, k, h * 128 + 64:h * 128 + 128], k == 0, k == 1))
                    MM(bank[:, :], mms)
                    CP("dve", VA[:, half * 8:(half + 1) * 8, ro[0]:ro[1]], bank[:, :].rearrange("p (t c) -> p t c", c=64))
                ych = 8 + (h // 2) % 2
                items = [(qb, kt) for qb in range(4) for kt in range(qb * 4 + 4)]
                stt = {}

                def mA(idx, it, h=h):
                    qb, kt = it
                    q0 = qb * 512
                    d = kt - q0 // 128
                    extra = [(d * 128, ident_b, negchk_b)] if d >= 0 else []
                    stt[idx] = sm_A(idx, QT, KT, (0, 96), kt, q0, sc_mla, 0.0, extra)

                def mB(idx, it, ro=ro, rd=rd, ych=ych):
                    qb, kt = it
                    q0 = qb * 512
                    ob = PB[2 + qb % 2]
                    P, c0, n = stt.pop(idx)
                    sm_B(P, c0, n, ob, None, VA[:, kt, :], kt == qb * 4 + 3, kt == 0)
                    if kt == qb * 4 + 3:
                        rc = SC[rd[0]:rd[1], 4, :]
                        RECIP(rc, ob[rd[0]:rd[1], :])
                        tq = SC[ro[0]:ro[1], 5, :]
                        TT("dve", tq, ob[ro[0]:ro[1], :], rc, ALU.mult)
                        TT("pool", sl(ych)[ro[0]:ro[1], q0:q0 + 512], tq, SZ[ro[0]:ro[1], q0:q0 + 512], ALU.mult)
                run_pipeline(items, [(mA, 0), (mB, 3)])
                if h % 4 == 3:
                    out_proj(wout0_d, h // 4, (8, 9))
            if list(layers) == [0, 1]:
                diff_prep(0.8 - 0.6 * math.exp(-0.3 * 1))
            QA, QB, KS, SZ2 = sl(11), sl(16), sl(12), sl(13)
            VS = SL[:, 14:16, :].rearrange("p a b -> p (a b)").rearrange("p (t h c) -> p t h c", h=2, c=128)
            MSET("pool", SL[:, 14:16, :], 0.0)
            MSET("pool", QA, 0.0)
            MSET("pool", QB, 0.0)
            sc_sb = 64.0 ** -0.5
            sbcols = lambda pr: (672 + pr * 128, 1184 + pr * 128, 1696 + pr * 128, 2208 + 512 + pr * 128)
            abuf = [SL[:, 17:19, :].rearrange("p a b -> p (a b)").rearrange("p (k n) -> p k n", n=512),
                    SL[:, 19:21, :].rearrange("p a b -> p (a b)").rearrange("p (k n) -> p k n", n=512)]
            if DEFER:
                mod_dma(1, 0, abuf[0], "ada0")
                mod_dma(1, 1, abuf[1], "ada1")
            wnext = load4(w0v, sbcols(0), 0) if PREFETCH else None
            for pr in range(4):
                wb = wnext if PREFETCH else load4(w0v, sbcols(pr), pr % 2)
                proj_fm2(QA, QB, wb, (0, 128))
                proj_fm(KS, wb, (128, 256), 128)
                proj_fm(SZ2, wb, (384, 512), 128, AF.Silu)
                for q4 in range(4):
                    bank = nbank()
                    mms = []
                    for t4 in range(4):
                        tt = q4 * 4 + t4
                        for k in range(8):
                            mms.append((bank[:, t4 * 128:(t4 + 1) * 128], SL[:, k, tt * 128:(tt + 1) * 128], wb[:, k, 256:384], k == 0, k == 7))
                    MM(bank[:, :], mms)
                    b3 = bank[:, :].rearrange("p (t c) -> p t c", c=128)
                    CP("dve", VS[:, q4 * 4:(q4 + 1) * 4, 0, 0:64], b3[:, :, 0:64])
                    CP("dve", VS[:, q4 * 4:(q4 + 1) * 4, 1, 64:128], b3[:, :, 64:128])
                if pr < 3 and PREFETCH:
                    wnext = load4(w0v, sbcols(pr + 1), (pr + 1) % 2)
                if pr % 2 == 1:
                    out_proj_load(wout0_d, 2 + pr // 2)
                ych = 8 + pr % 2
                items = []
                for qb in range(4):
                    for hh in range(2):
                        nk = qb * 4 + 4
                        for kt in range(nk - 1, -1, -1):
                            items.append((qb, hh, kt, kt == nk - 1, hh == 1 and kt == 0, qb * 2 + hh))
                S21F = SL[:, 21, :].bitcast(F32)
                EB = [sc(0), sc(1), sc(2), S21F[:, 0:512], S21F[:, 512:1024]]

                def geo(it):
                    qb, hh, kt, first, last, chain = it
                    q0 = qb * 512
                    d = kt - q0 // 128
                    c0 = max(d, 0) * 128
                    return q0, d, c0, 512 - c0

                def s_z(idx, it):
                    qb, hh, kt, first, last, chain = it
                    q0, d, c0, n = geo(it)
                    rows = (hh * 64, hh * 64 + 64)
                    if first:
                        MSET("pool", sc(5 + chain % 2), 0.0)
                    zb = PB[idx % 2]
                    mms = [(zb[:, c0:512], KS[:, kt * 128:(kt + 1) * 128], (QA if hh == 0 else QB)[:, q0 + c0:q0 + 512], True, d < 0)]
                    if d >= 0:
                        mms.append((zb[:, c0:c0 + 128], ident_b, negtri_b, False, True))
                    MM(zb[:, c0:512], mms)
                    ACT(EB[idx % 5][:, 0:n], zb[:, c0:512], AF.Exp, scale=sc_sb)

                def s_x(idx, it):
                    q0, d, c0, n = geo(it)
                    csf = sc(3 + idx % 2)
                    ACT(csf[:, 0:n], csf[:, 0:n], AF.Exp, scale=-1.0)
                    W = SL[:, 22, (idx % 2) * 512:(idx % 2) * 512 + 512]
                    TT("pool", W[:, 0:n], EB[idx % 5][:, 0:n], csf[:, 0:n], ALU.mult)

                def s_sp(idx, it):
                    q0, d, c0, n = geo(it)
                    SP = SL[:, 22, (2 + idx % 2) * 512:(2 + idx % 2) * 512 + 512]
                    ACT(SP[:, 0:n], EB[idx % 5][:, 0:n], AF.Ln, bias=1.0)

                def s_cs(idx, it):
                    qb, hh, kt, first, last, chain = it
                    q0, d, c0, n = geo(it)
                    carry = sc(5 + chain % 2)
                    SP = SL[:, 22, (2 + idx % 2) * 512:(2 + idx % 2) * 512 + 512]
                    cb, tb = PB[3 + idx % 2], PB[5 + 2 * (idx % 2)]
                    MM(cb[:, 0:n], [(cb[:, 0:n], U_b, SP[:, 0:n], True, True)])
                    MM(tb[:, 0:n], [(tb[:, 0:n], ones_b, SP[:, 0:n], True, True)])
                    if WARM_DUMMY:
                        MM(PB[6][:, :], [(PB[6][:, :], ones_b, SL[:, 0, 0:512], True, True)] * WARM_DUMMY)
                    csf = sc(3 + idx % 2)
                    TT("dve", csf[:, 0:n], cb[:, 0:n], carry[:, c0:512], ALU.add)
                    TT("dve", carry[:, c0:512], tb[:, 0:n], carry[:, c0:512], ALU.add)

                def s_pv(idx, it, ych=ych):
                    qb, hh, kt, first, last, chain = it
                    q0, d, c0, n = geo(it)
                    W = SL[:, 22, (idx % 2) * 512:(idx % 2) * 512 + 512]
                    ob = PB[2]
                    if first and hh == 0:
                        zinit(ob)
                    MM(ob[:, c0:512], [(ob[:, c0:512], VS[:, kt, hh, :], W[:, 0:n], False, last)])
                    if last:
                        TT("dve", sl(ych)[:, q0:q0 + 512], ob[:, :], SZ2[:, q0:q0 + 512], ALU.mult)
                run_pipeline(items, [(s_z, 0), (s_x, 2), (s_sp, 0), (s_cs, 1), (s_pv, 3)])
                if DEFER and pr < 3:
                    mod_mm(1, 2 * pr, abuf[0])
                    mod_mm(1, 2 * pr + 1, abuf[1])
                    if pr < 2:
                        mod_dma(1, 2 * pr + 2, abuf[0], "ada0")
                        mod_dma(1, 2 * pr + 3, abuf[1], "ada1")
                    else:
                        mod_fin(1)
                if pr % 2 == 1:
                    out_proj(wout0_d, 2 + pr // 2, (8, 9))

        def layer_diff(l, i_layer):
            lam_init = 0.8 - 0.6 * math.exp(-0.3 * i_layer)
            norm_phase(l)
            gate_bc(l)
            diff_prep(lam_init)
            w1v = win1_d.rearrange("(k p) n -> p k n", p=128)
            QD0, QD1, KD, SZ = sl(11), sl(15), sl(12), sl(14)
            VD = sl(13).rearrange("p (t c) -> p t c", c=128)
            MSET("pool", QD0, 0.0)
            MSET("pool", QD1, 0.0)
            sc_d = 64.0 ** -0.5
            dcols = lambda h: (h * 128, 1024 + h * 128, 2048 + h * 128, 3072 + h * 128)
            wnext = load4(w1v, dcols(0), 0) if PREFETCH else None
            for h in range(8):
                wb = wnext if PREFETCH else load4(w1v, dcols(h), h % 2)
                proj_fm2(QD0, QD1, wb, (0, 128))
                proj_fm(KD, wb, (128, 256), 128)
                proj_fm(SZ, wb, (384, 512), 128, AF.Silu)
                for q4 in range(4):
                    bank = nbank()
                    mms = []
                    for t4 in range(4):
                        tt = q4 * 4 + t4
                        for k in range(8):
                            mms.append((bank[:, t4 * 128:(t4 + 1) * 128], SL[:, k, tt * 128:(tt + 1) * 128], wb[:, k, 256:384], k == 0, k == 7))
                    MM(bank[:, :], mms)
                    CP("dve", VD[:, q4 * 4:(q4 + 1) * 4, :], bank[:, :].rearrange("p (t c) -> p t c", c=128))
                if h < 7 and PREFETCH:
                    wnext = load4(w1v, dcols(h + 1), (h + 1) % 2)
                if h % 2 == 1:
                    out_proj_load(wout1_d, h // 2)
                ych = 8 + h % 2
                obs = [PB[2], PB[3]]
                dbs = [PB[4], PB[5]]
                items = [(qb, c, kt) for qb in range(4) for c in range(2) for kt in range(qb * 4 + 4)]
                stt = {}

                def dA(idx, it, h=h):
                    qb, c, kt = it
                    q0 = qb * 512
                    d = kt - q0 // 128
                    extra = []
                    for e4 in range(max(d, 0), 4):
                        dl = d - e4
                        if dl == 0:
                            extra.append((e4 * 128, anti_b, HK[:, h, 0, :]))
                            extra.append((e4 * 128, ident_b, negchk_b))
                        elif dl == -1:
                            extra.append((e4 * 128, anti_b, HK[:, h, 1, :]))
                    stt[idx] = sm_A(idx, QD0 if c == 0 else QD1, KD, (0, 128), kt, q0, sc_d, 0.0, extra, ZB3)

                def dB(idx, it, ych=ych):
                    qb, c, kt = it
                    q0 = qb * 512
                    P, c0, n = stt.pop(idx)
                    last = kt == qb * 4 + 3
                    sm_B(P, c0, n, obs[c], dbs[c], VD[:, kt, :], last, kt == 0)
                    if c == 1 and last:
                        r0, r1, a0, a1 = sc(0), sc(1), sc(2), sc(3)
                        sq = SL[:, 21, 0:512]
                        yv = sl(ych)[:, q0:q0 + 512]
                        szv = SZ[:, q0:q0 + 512]
                        ACT(r0, dbs[0][:, :], AF.Ln)
                        CP("dve", a0, obs[0][:, :])
                        ACT(r1, dbs[1][:, :], AF.Ln)
                        CP("dve", a1, obs[1][:, :])

                        def e1():
                            ACT(r0, r0, AF.Exp, scale=-1.0)
                            ACT(r1, r1, AF.Exp, scale=-1.0)

                        def e2():
                            TT("dve", a0, a0, r0, ALU.mult)
                            TT("dve", a1, a1, r1, ALU.mult)
                            STT("dve", a0, a1, NLAM, a0, ALU.mult, ALU.add)

                        def e3():
                            ACT(sq, a0, AF.Square)
                            MM(PB[6][:, :], [(PB[6][:, :], ones_b, sq, True, True)])

                        def e4():
                            TS("dve", r0, PB[6][:, :], 1.0 / 128, 1e-5, ALU.mult, ALU.add)

                        def e5():
                            ACT(r0, r0, AF.Ln)

                        def e6():
                            ACT(r0, r0, AF.Exp, scale=-0.5)

                        def e7():
                            TT("dve", a0, a0, r0, ALU.mult)
                            STT("dve", yv, a0, SGP, szv, ALU.mult, ALU.mult)
                        defer(1, e1)
                        defer(2, e2)
                        defer(4, e3)
                        defer(5, e4)
                        defer(6, e5)
                        defer(7, e6)
                        defer(8, e7)
                run_pipeline(items, [(dA, 0), (dB, 3)])
                if h % 2 == 1:
                    out_proj(wout1_d, h // 2, (8, 9))

        for l in layers:
            if l % 2 == 0:
                layer_ab(l)
            else:
                layer_diff(l, l)
        if final:
            S.dma("sp", GBC[:], fg_d, "fg")
            SSQ = SM[:, 176:192]
            RSTD = SM[:, 160:176]
            MSET("pool", SSQ, 0.0)
            for tt in range(NT):
                ACT(sc2(0), X[:, tt, :], AF.Square, accum=SSQ[:, tt:tt + 1])
            TS("dve", SSQ, SSQ, 1.0 / D, 1e-6, ALU.mult, ALU.add)
            ACT(SSQ, SSQ, AF.Ln)
            ACT(RSTD, SSQ, AF.Exp, scale=-0.5)
            for tt in range(NT):
                STT("dve", X[:, tt, :], X[:, tt, :], RSTD[:, tt:tt + 1], GBC[:], ALU.mult, ALU.mult)
                S.dma("sp", out_d[tt * 128:(tt + 1) * 128, :], X[:, tt, :], "o%d" % (tt % 4), sb_out=False, sb_in=True)
        else:
            for tt in range(NT):
                S.dma("sp", out_d[tt * 128:(tt + 1) * 128, :], X[:, tt, :], "o%d" % (tt % 4), sb_out=False, sb_in=True)
        S.emit(st)
    return nc


_CACHE = {}
FUSED = True
DEFER_ADA = True
PREFETCH = True
WARM_DUMMY = 0


def _in_maps(inp, xs):
    cst = _consts()
    f = lambda a: np.ascontiguousarray(np.asarray(a, dtype=np.float32))
    maps = []
    for b in range(8):
        m = {
            "x": f(xs[b]),
            "c_t": f(np.asarray(inp["c"])[b].reshape(8, 128).T),
            "posi": np.full((128, 1), int(np.asarray(inp["pos_offset"])[b]), np.int32),
            "ada_w": f(inp["ada_w"]),
            "ada_b_t": f(np.asarray(inp["ada_b"]).reshape(2, 24, 128).transpose(2, 0, 1)),
            "norm_g_t": f(np.asarray(inp["norm_g"]).reshape(2, 8, 128).transpose(2, 0, 1)),
            "final_g_bc": f(np.broadcast_to(np.asarray(inp["final_g"])[None, :], (128, D))),
            "ab_w_in": f(np.asarray(inp["ab_w_in"])[0]),
            "gq_t": f(np.asarray(inp["ab_q_norm_g"])[0].reshape(3, 128).T),
            "gkv_t": f(np.asarray(inp["ab_kv_norm_g"])[0].reshape(2, 128).T),
            "ab_w_uq": f(np.asarray(inp["ab_w_uq"])[0]),
            "ab_w_ukv": f(np.asarray(inp["ab_w_ukv"])[0]),
            "ab_w_out": f(np.asarray(inp["ab_w_out"])[0]),
            "dif_w_in": f(np.asarray(inp["dif_w_in"])[0]),
            "lamv": f(np.broadcast_to(np.stack([np.asarray(inp[k])[0] for k in ("dif_lam_q1", "dif_lam_k1", "dif_lam_q2", "dif_lam_k2")])[None], (128, 4, 64))),
            "subln_t": f(np.asarray(inp["dif_subln_g"])[0].reshape(128, 1)),
            "dif_w_out": f(np.asarray(inp["dif_w_out"])[0]),
            "rel_tab": f(inp["rel_bias_table"]),
        }
        m.update(cst)
        maps.append(m)
    return maps


def _run(layers, final, inp, xs):
    key = (tuple(layers), final)
    if key not in _CACHE:
        _CACHE[key] = build(list(layers), final)
    res = run_bass_kernel_spmd(_CACHE[key], _in_maps(inp, xs), core_ids=list(range(8)))
    return np.stack([r["out"] for r in res.results], 0)


def kernel(**inp):
    x = np.asarray(inp["x"], dtype=np.float32)
    if FUSED:
        return _run((0, 1), True, inp, x)
    x1 = _run((0,), False, inp, x)
    return _run((1,), True, inp, x1)
```

```python
import math
import numpy as np
from contextlib import ExitStack
import concourse.bass as bass
import concourse.mybir as mybir
from concourse.bass_utils import run_bass_kernel_spmd

F32 = mybir.dt.float32
BF16 = mybir.dt.bfloat16
I32 = mybir.dt.int32
AF = mybir.ActivationFunctionType
ALU = mybir.AluOpType

_DTS = {F32: 4, BF16: 2, I32: 4}


def _region(ap):
    pat = ap.ap
    ds = _DTS.get(ap.dtype, 4)
    pstride = pat[0][0]
    off = ap.offset
    if pstride == 0:
        p0, f0, np_ = 0, off, 1
    else:
        p0, f0, np_ = off // pstride, off % pstride, pat[0][1]
    lo = hi = f0
    for st, cnt in pat[1:]:
        if st >= 0:
            hi += st * (cnt - 1)
        else:
            lo += st * (cnt - 1)
    if ap.tensor.name.startswith("PB"):
        return (ap.tensor.name, 0, 128, 0, 2048)
    return (ap.tensor.name, p0, p0 + np_, lo * ds, (hi + 1) * ds)


class Op:
    __slots__ = ("eng", "fn", "deps", "idx", "sig", "signum", "dma", "dsem", "dval")


class Sched:
    ENG = ("pe", "act", "dve", "pool", "sp")

    def __init__(self, nc):
        self.nc = nc
        self.ops = []
        self.rec = {}
        self.dkeys = {}

    def _deps_for(self, idx, eng, is_dma, reads, writes):
        deps = set()
        for ap in reads:
            name, p0, p1, b0, b1 = _region(ap)
            lst = self.rec.setdefault(name, [])
            psum = name.startswith("PB")
            for r in lst:
                if r[5] and r[0] < p1 and p0 < r[1] and r[2] < b1 and b0 < r[3]:
                    deps.add(r[4])
                elif psum and not r[5] and self.ops[r[4]].eng != eng:
                    deps.add(r[4])
            key = (p0, p1, b0, b1)
            for r in lst:
                if (not r[5]) and (r[0], r[1], r[2], r[3]) == key:
                    o = self.ops[r[4]]
                    if o.eng == eng and not o.dma and not is_dma:
                        r[4] = idx
                        break
            else:
                lst.append([p0, p1, b0, b1, idx, False])
        for ap in writes:
            name, p0, p1, b0, b1 = _region(ap)
            lst = self.rec.setdefault(name, [])
            keep = []
            for r in lst:
                if r[0] < p1 and p0 < r[1] and r[2] < b1 and b0 < r[3]:
                    if r[4] != idx:
                        deps.add(r[4])
                    if r[0] >= p0 and r[1] <= p1 and r[2] >= b0 and r[3] <= b1 and r[4] != idx:
                        continue
                keep.append(r)
            keep.append([p0, p1, b0, b1, idx, True])
            self.rec[name] = keep
        deps.discard(idx)
        return deps

    def op(self, eng, fn, outs, ins):
        o = Op()
        o.eng, o.fn, o.idx, o.dma, o.sig, o.signum = eng, fn, len(self.ops), False, False, 0
        o.dsem = o.dval = None
        self.ops.append(o)
        o.deps = self._deps_for(o.idx, eng, False, ins, outs)
        return o

    def dma(self, queue, out, in_, key, sb_out=True, sb_in=False):
        o = Op()
        o.eng, o.idx, o.dma, o.sig, o.signum = queue, len(self.ops), True, False, 0
        o.fn = lambda e: e.dma_start(out=out, in_=in_)
        self.ops.append(o)
        o.deps = self._deps_for(o.idx, queue, True, [in_] if sb_in else [], [out] if sb_out else [])
        st = self.dkeys.setdefault(key, [None, 0])
        if st[0] is not None:
            o.deps.add(st[0])
        st[0] = o.idx
        st[1] += 16
        o.dsem, o.dval = key, st[1]
        return o

    def emit(self, stack):
        nc = self.nc
        engs = {"pe": nc.tensor, "act": nc.scalar, "dve": nc.vector, "pool": nc.gpsimd, "sp": nc.sync}
        ops = self.ops
        known = {e: {} for e in self.ENG}
        vcs = [None] * len(ops)
        plan = []
        self.nwaits = 0

        def merge(kn, vc):
            for k, v in vc.items():
                if kn.get(k, -1) < v:
                    kn[k] = v
        for o in ops:
            waits_e = {}
            waits_d = {}
            for d in o.deps:
                p = ops[d]
                if p.dma:
                    if waits_d.get(p.dsem, (0, -1))[0] < p.dval:
                        waits_d[p.dsem] = (p.dval, d)
                else:
                    if p.eng == o.eng and not o.dma and o.eng == "pe":
                        continue
                    if waits_e.get(p.eng, -1) < d:
                        waits_e[p.eng] = d
            kn = known[o.eng]
            we = {}
            for e, d in sorted(waits_e.items(), key=lambda x: -x[1]):
                if kn.get(("e", e), -1) >= d:
                    continue
                kn[("e", e)] = d
                merge(kn, vcs[d])
                ops[d].sig = True
                we[e] = d
            wd = {}
            for k, (v, d) in waits_d.items():
                if kn.get(("d", k), 0) >= v:
                    continue
                kn[("d", k)] = v
                merge(kn, vcs[d])
                wd[k] = v
            self.nwaits += len(we) + len(wd)
            vc = dict(kn)
            if o.dma:
                vc[("d", o.dsem)] = o.dval
            else:
                vc[("e", o.eng)] = max(vc.get(("e", o.eng), -1), o.idx)
            vcs[o.idx] = vc
            plan.append((we, wd))
        cnt = {e: 0 for e in self.ENG}
        for o in ops:
            if o.sig and not o.dma:
                cnt[o.eng] += 1
                o.signum = cnt[o.eng]
        esem = {e: stack.enter_context(nc.semaphore("s_" + e)) for e in self.ENG}
        dsem = {k: stack.enter_context(nc.semaphore("d_" + str(k))) for k in self.dkeys}
        for o, (we, wd) in zip(ops, plan):
            eh = engs[o.eng]
            for e, d in we.items():
                eh.wait_ge(esem[e], ops[d].signum)
            for k, v in wd.items():
                eh.wait_ge(dsem[k], v)
            ins = o.fn(eh)
            if o.dma:
                ins.then_inc(dsem[o.dsem], 16)
            elif o.sig:
                ins.then_inc(esem[o.eng], 1)
        for k, st in self.dkeys.items():
            nc.sync.wait_ge(dsem[k], st[1])
        return cnt


D = 1024
S_ = 2048
NT = 16
NEG = -30000.0


def _t5_bucket(rel):
    nb, me = 16, 8
    ret = np.where(rel > 0, nb, 0)
    n = np.abs(rel)
    nf = np.maximum(n, 1).astype(np.float32)
    large = me + (np.log(nf / me) / math.log(128 / me) * (nb - me)).astype(np.int32)
    large = np.minimum(large, nb - 1)
    return ret + np.where(n < me, n, large)


def _consts():
    c = {}
    j = np.arange(128)[:, None]
    t = np.arange(128)[None, :]
    m = np.zeros((128, 8, 128), np.float32)
    m[:, 0] = np.eye(128)
    m[:, 1] = (j == 127 - t)
    m[:, 2] = (j >= t)
    m[:, 3] = 1.0
    m[:, 4] = 0.0
    m[:, 5] = np.where(j < t, 0.0, NEG)
    m[:, 6] = np.where((j // 64) <= (t // 64), 0.0, NEG)
    c["cmat"] = m.reshape(128, 1024)
    cc = np.zeros((128, 8), np.float32)
    p = np.arange(128)
    cc[:, 0] = 10000.0 ** (-(p % 16) / 16.0)
    cc[:, 1] = np.where((p % 32) < 16, -1.0, 1.0)
    c["ccol"] = cc
    oh = np.zeros((32, 384), np.float32)
    i = np.arange(383)
    b = _t5_bucket(127 - i)
    oh[b, i] = 1.0
    oh[15, :383] -= 1.0
    oh *= 8.0
    c["oh1d"] = oh
    sel = np.zeros((32, 128), np.float32)
    sel[15, :] = 1.0
    c["sel15"] = sel
    c["trow"] = np.arange(S_, dtype=np.float32)[None, :].repeat(128, 0)
    return c


def build(layers, final):
    nc = bass.Bass("TRN2", target_bir_lowering=False)
    dt = lambda n, s, d=F32, k="ExternalInput": nc.dram_tensor(n, s, d, kind=k).ap()
    x_d = dt("x", [S_, D])
    out_d = dt("out", [S_, D], F32, "ExternalOutput")
    ct_d = dt("c_t", [128, 8])
    pos_d = dt("posi", [128, 1], I32)
    adaw_d = dt("ada_w", [2, D, 3 * D])
    adab_d = dt("ada_b_t", [128, 2, 24])
    ng_d = dt("norm_g_t", [128, 2, 8])
    fg_d = dt("final_g_bc", [128, D])
    win0_d = dt("ab_w_in", [D, 3232])
    gq_d = dt("gq_t", [128, 3])
    gkv_d = dt("gkv_t", [128, 2])
    wuq_d = dt("ab_w_uq", [384, 768])
    wukv_d = dt("ab_w_ukv", [256, 1024])
    wout0_d = dt("ab_w_out", [D, D])
    win1_d = dt("dif_w_in", [D, 4096])
    lamv_d = dt("lamv", [128, 4, 64])
    sg_d = dt("subln_t", [128, 1])
    wout1_d = dt("dif_w_out", [D, D])
    tab_d = dt("rel_tab", [32, 8])
    cmat_d = dt("cmat", [128, 1024])
    ccol_d = dt("ccol", [128, 8])
    oh_d = dt("oh1d", [32, 384])
    sel_d = dt("sel15", [32, 128])
    trow_d = dt("trow", [128, S_])
    gscr = nc.dram_tensor("gscr", [8, 384], F32, kind="Internal").ap()

    st = ExitStack()
    with st:
        sbt = lambda n, s, d: st.enter_context(nc.sbuf_tensor(n, s, d))
        X = sbt("X", [128, NT, D], F32)
        NSL = 27
        SL = sbt("SL", [128, NSL, 2048], BF16)
        MW = sbt("MW", [128, 5376], BF16)
        GBC = sbt("GBC", [128, D], F32)
        NSC = 7
        SC = sbt("SC", [128, NSC, 512], F32)
        CB = sbt("CB", [128, 8, 128], BF16)
        CF = sbt("CF", [128, 2, 128], F32)
        CC = sbt("CC", [128, 8], F32)
        SM = sbt("SM", [128, 192], F32)
        SMB = sbt("SMB", [128, 8], BF16)
        TB = sbt("TB", [32, 8 + 384 + 128], F32)
        POSI = sbt("POSI", [128, 1], I32)
        LAMV = sbt("LAMV", [128, 4, 64], F32)
        PB = [st.enter_context(nc.psum_tensor("PB%d" % i, [128, 512], F32)) for i in range(8)]

        S = Sched(nc)
        sl = lambda i: SL[:, i, :]
        ident_b, anti_b, U_b, ones_b, zero_b, negtri_b, negchk_b = [CB[:, i, :] for i in range(7)]
        ident_f, ones_f = CF[:, 0, :], CF[:, 1, :]

        def ACT(out, in_, func, scale=1.0, bias=0.0, accum=None, extra_in=()):
            kw = {}
            if accum is not None:
                kw["accum_out"] = accum
            ins = [in_] + list(extra_in)
            if not isinstance(scale, (int, float)):
                ins.append(scale)
            if not isinstance(bias, (int, float)):
                ins.append(bias)
            outs = [out] + ([accum] if accum is not None else [])
            if not (isinstance(scale, (int, float)) and scale == 1.0):
                kw["scale"] = scale
            if not (isinstance(bias, (int, float)) and bias == 0.0):
                kw["bias"] = bias
            return S.op("act", lambda e: e.activation(out=out, in_=in_, func=func, **kw), outs, ins)

        def TT(eng, out, in0, in1, op):
            return S.op(eng, lambda e: e.tensor_tensor(out=out, in0=in0, in1=in1, op=op), [out], [in0, in1])

        def TS(eng, out, in0, s1, s2, op0, op1=None):
            ins = [in0] + [s for s in (s1, s2) if s is not None and not isinstance(s, (int, float))]
            if op1 is None:
                return S.op(eng, lambda e: e.tensor_scalar(out=out, in0=in0, scalar1=s1, scalar2=None, op0=op0), [out], ins)
            return S.op(eng, lambda e: e.tensor_scalar(out=out, in0=in0, scalar1=s1, scalar2=s2, op0=op0, op1=op1), [out], ins)

        def STT(eng, out, in0, sc, in1, op0, op1):
            ins = [in0, in1] + ([sc] if not isinstance(sc, (int, float)) else [])
            return S.op(eng, lambda e: e.scalar_tensor_tensor(out=out, in0=in0, scalar=sc, in1=in1, op0=op0, op1=op1), [out], ins)

        def CP(eng, out, in_):
            return S.op(eng, lambda e: e.tensor_copy(out=out, in_=in_), [out], [in_])

        def MSET(eng, out, v):
            return S.op(eng, lambda e: e.memset(out, v), [out], [])

        def RECIP(out, in_):
            return S.op("dve", lambda e: e.reciprocal(out=out, in_=in_), [out], [in_])

        def MM(outreg, mms):
            ins = []
            for (_, l, r, _, _) in mms:
                ins += [l, r]

            def fn(e):
                last = None
                for (o, l, r, a, b) in mms:
                    last = e.matmul(o, lhsT=l, rhs=r, start=a, stop=b)
                return last
            return S.op("pe", fn, [outreg], ins)

        def TR(outreg, trs):
            ins = []
            for (_, i, idn) in trs:
                ins += [i, idn]

            def fn(e):
                last = None
                for (o, i, idn) in trs:
                    last = e.transpose(o, i, idn)
                return last
            return S.op("pe", fn, [outreg], ins)

        dq = [0]

        def wdma(out, in_, key):
            return S.dma("pool", out, in_, key)

        def sc(i, w=512):
            return SC[:, i, 0:w]

        def sc2(i):
            return SC[:, i:i + 2, :].rearrange("p a b -> p (a b)")

        CST = SC[:, 0:2, :].rearrange("p a b -> p (a b)")
        S.dma("sp", CST, cmat_d, "c0")
        S.dma("sp", CC[:], ccol_d, "c1")
        S.dma("sp", TB[:, 0:8], tab_d, "c2")
        S.dma("sp", TB[:, 8:392], oh_d, "c2")
        S.dma("sp", TB[:, 392:520], sel_d, "c2")
        S.dma("sp", POSI[:], pos_d, "c1")
        S.dma("sp", LAMV[:], lamv_d, "c1")
        S.dma("sp", SM[:, 0:8], ct_d, "c3")
        S.dma("sp", SM[:, 8:56].rearrange("p (a b) -> p a b", b=24), adab_d, "c3")
        S.dma("sp", SM[:, 56:72].rearrange("p (a b) -> p a b", b=8), ng_d, "c3")
        S.dma("sp", SM[:, 72:75], gq_d, "c3")
        S.dma("sp", SM[:, 75:77], gkv_d, "c3")
        S.dma("sp", SM[:, 77:78], sg_d, "c3")
        CT, ADAB, NG, GQ, GKV, SG = SM[:, 0:8], SM[:, 8:56], SM[:, 56:72], SM[:, 72:75], SM[:, 75:77], SM[:, 77:78]
        MODC = SM[:, 80:128]
        GMOD = SM[:, 128:144]
        POSF = SM[:, 144:145]
        NLAM = SM[:, 145:146]
        LTMP = SM[:, 146:150]
        CBIAS = SM[:, 150:158]
        CP("dve", CB[:].rearrange("p a b -> p (a b)"), CST)
        CP("dve", CF[:, 0, :], CST[:, 0:128])
        CP("dve", CF[:, 1, :], CST[:, 384:512])
        if 0 in layers:
            COS, SIN = sl(15), sl(16)
            MSET("pool", SM[:, 159:160], math.pi / 2)
            CP("dve", POSF, POSI[:])
            for nb in range(4):
                ang = sc(0)
                S.dma("sp", ang, trow_d[:, nb * 512:(nb + 1) * 512], "trow")
                TS("dve", ang, ang, POSF, CC[:, 0:1], ALU.add, ALU.mult)
                kf, ki = sc(1), sc(3).bitcast(I32)
                TS("dve", kf, ang, 1.0 / (2 * math.pi), None, ALU.mult)
                CP("dve", ki, kf)
                CP("dve", kf, ki)
                STT("dve", ang, kf, -2.0 * math.pi, ang, ALU.mult, ALU.add)
                s2, c2 = sc(1), sc(2)
                ACT(s2, ang, AF.Sin, scale=0.5)
                ACT(c2, ang, AF.Sin, scale=-0.5, bias=SM[:, 159:160])
                STT("dve", c2, s2, 2.0, c2, ALU.mult, ALU.mult)
                TS("dve", SIN[:, nb * 512:(nb + 1) * 512], c2, CC[:, 1:2], None, ALU.mult)
                TT("dve", s2, s2, s2, ALU.mult)
                TS("dve", COS[:, nb * 512:(nb + 1) * 512], s2, -2.0, 1.0, ALU.mult, ALU.add)

        ada_stg = {}
        _adv0 = adaw_d.rearrange("l (k p) n -> l p k n", p=128)
        for nb, s0 in ((2, 8), (3, 17)):
            ada_stg[nb] = SL[:, s0:s0 + 4, :].rearrange("p a b -> p (a b)").bitcast(F32).rearrange("p (k n) -> p k n", n=512)
            S.dma("sp", ada_stg[nb], _adv0[layers[0], :, :, nb * 512:(nb + 1) * 512], "adf%d" % nb)
        for tt in range(NT):
            S.dma("sp", X[:, tt, :], x_d[tt * 128:(tt + 1) * 128, :], "x%d" % (tt % 4))

        ACT(SMB[:, 0:8], CT, AF.Silu)
        CACT = SMB[:, 0:8]
        adv = adaw_d.rearrange("l (k p) n -> l p k n", p=128)
        DEFER = DEFER_ADA and (list(layers) == [0, 1])

        def mod_dma(l, nb, wb, key):
            wdma(wb, adv[l, :, :, nb * 512:(nb + 1) * 512], key)

        def mod_mm(l, nb, wb):
            mms = []
            for jj in range(4):
                col = PB[6][:, nb * 4 + jj:nb * 4 + jj + 1]
                for k in range(8):
                    mms.append((col, wb[:, k, jj * 128:(jj + 1) * 128], CACT[:, k:k + 1], k == 0, k == 7))
            MM(PB[6][:, 0:32], mms)
            o_ = MODC[:, l * 24 + nb * 4:l * 24 + nb * 4 + 4]
            i0_, i1_ = PB[6][:, nb * 4:nb * 4 + 4], ADAB[:, l * 24 + nb * 4:l * 24 + nb * 4 + 4]
            S.op("dve", lambda e: e.tensor_tensor(out=o_, in0=i0_, in1=i1_, op=ALU.add), [o_], [PB[6][:, 0:32], i1_])

        def mod_fin(l):
            TS("dve", GMOD[:, l * 8:(l + 1) * 8], MODC[:, l * 24 + 8:l * 24 + 16], 1.0, None, ALU.add)
            TT("dve", GMOD[:, l * 8:(l + 1) * 8], GMOD[:, l * 8:(l + 1) * 8], NG[:, l * 8:(l + 1) * 8], ALU.mult)

        def stream_buf(bi):
            return SL[:, 23 + 2 * bi:25 + 2 * bi, :].rearrange("p a b -> p (a b)").rearrange("p (k n) -> p k n", n=512)

        blk_i = [0]
        for li, l in enumerate(layers[:1] if DEFER else layers):
            if li == 0:
                pb4 = [SL[:, 2 * q:2 * q + 2, :].rearrange("p a b -> p (a b)").rearrange("p (k n) -> p k n", n=512) for q in range(4)]
                for nb in range(2):
                    mod_dma(l, nb, pb4[nb], "adp%d" % nb)
                for nb in (2, 3):
                    CP("dve", pb4[nb], ada_stg[nb])
                for nb in (4, 5):
                    mod_dma(l, nb, stream_buf(nb % 2), "wst%d" % (nb % 2))
                for nb in range(6):
                    mod_mm(l, nb, pb4[nb] if nb < 4 else stream_buf(nb % 2))
            else:
                for nb in range(6):
                    bi = blk_i[0] % 2
                    blk_i[0] += 1
                    mod_dma(l, nb, stream_buf(bi), "wst%d" % bi)
                    mod_mm(l, nb, stream_buf(bi))
            mod_fin(l)


        def norm_phase(l):
            SS = SM[:, 158:159]
            RSTD = SM[:, 160:176]
            SSQ = SM[:, 176:192]
            MSET("pool", SSQ, 0.0)
            for tt in range(NT):
                ACT(sc2(0), X[:, tt, :], AF.Square, accum=SSQ[:, tt:tt + 1])
            TS("dve", SSQ, SSQ, 1.0 / D, 1e-6, ALU.mult, ALU.add)
            ACT(SSQ, SSQ, AF.Ln)
            ACT(RSTD, SSQ, AF.Exp, scale=-0.5)
            def xs_of(tt):
                return sc2(2 * (tt % 2))

            ACT(xs_of(0), X[:, 0, :], AF.Identity, scale=RSTD[:, 0:1])
            for tt in range(NT):
                xs = xs_of(tt)
                if tt + 1 < NT:
                    ACT(xs_of(tt + 1), X[:, tt + 1, :], AF.Identity, scale=RSTD[:, tt + 1:tt + 2])
                for half in range(2):
                    bank = PB[(tt * 2 + half) % 4]
                    TR(bank[:, :], [(bank[:, c * 128:(c + 1) * 128], xs[:, (half * 4 + c) * 128:(half * 4 + c + 1) * 128], ident_f) for c in range(4)])
                    for c in range(4):
                        cc = half * 4 + c
                        if half == 0:
                            TS("dve", SL[:, cc, tt * 128:(tt + 1) * 128], bank[:, c * 128:(c + 1) * 128],
                               GMOD[:, l * 8 + cc:l * 8 + cc + 1], MODC[:, l * 24 + cc:l * 24 + cc + 1], ALU.mult, ALU.add)
                        else:
                            ACT(SL[:, cc, tt * 128:(tt + 1) * 128], bank[:, c * 128:(c + 1) * 128], AF.Identity,
                                scale=GMOD[:, l * 8 + cc:l * 8 + cc + 1], bias=MODC[:, l * 24 + cc:l * 24 + cc + 1])

        def gate_bc(l):
            for c in range(8):
                dg = SC[:, 6, 0:128]
                TS("dve", dg, ident_f, MODC[:, l * 24 + 16 + c:l * 24 + 17 + c], None, ALU.mult)
                MM(PB[7][:, (c % 4) * 128:(c % 4 + 1) * 128], [(PB[7][:, (c % 4) * 128:(c % 4 + 1) * 128], ones_f, dg, True, True)])
                CP("dve", GBC[:, c * 128:(c + 1) * 128], PB[7][:, (c % 4) * 128:(c % 4 + 1) * 128])

        _rot = [0]
        ROT = [6, 7, 0, 1, 4, 5]

        def nbank():
            b = PB[ROT[_rot[0] % len(ROT)]]
            _rot[0] += 1
            return b

        def out_proj_load(wout_d, g):
            wo = SL[:, 10, :].rearrange("p (c n) -> p c n", n=1024)
            wv = wout_d.rearrange("(c p) n -> p c n", p=128)
            wdma(wo, wv[:, 2 * g:2 * g + 2, :], "wout")
            for c in range(2):
                TT("dve", wo[:, c, :], wo[:, c, :], GBC[:], ALU.mult)

        def out_proj(wout_d, g, ychunks):
            wo = SL[:, 10, :].rearrange("p (c n) -> p c n", n=1024)
            for tt in range(NT):
                for nb in range(2):
                    bank = PB[[6, 7, 0, 1][(tt * 2 + nb) % 4]]
                    MM(bank[:, :], [(bank[:, :], sl(ychunks[c])[:, tt * 128:(tt + 1) * 128], wo[:, c, nb * 512:(nb + 1) * 512], c == 0, c == 1) for c in range(2)])
                    xv = X[:, tt, nb * 512:(nb + 1) * 512]
                    ti = tt * 2 + nb
                    if ti % 3 == 2:
                        tmp = sc(ti // 3 % 2)
                        ACT(tmp, bank[:, :], AF.Copy)
                        TT("pool", xv, xv, tmp, ALU.add)
                    else:
                        TT("dve", xv, xv, bank[:, :], ALU.add)

        def proj_fm2(dA, dB, w, cols):
            for nb in range(4):
                bank = nbank()
                MM(bank[:, :], [(bank[:, :], w[:, k, cols[0]:cols[1]], SL[:, k, nb * 512:(nb + 1) * 512], k == 0, k == 7) for k in range(8)])
                ACT(dA[0:64, nb * 512:(nb + 1) * 512], bank[0:64, :], AF.Copy)
                ACT(dB[64:128, nb * 512:(nb + 1) * 512], bank[64:128, :], AF.Copy)

        def load4(wview, cols4, bi):
            wb = stream_buf(bi)
            for i4, c0 in enumerate(cols4):
                wdma(wb[:, :, i4 * 128:(i4 + 1) * 128], wview[:, :, c0:c0 + 128], "wst%d" % bi)
            return wb

        def proj_fm(dst, w, cols, nrows, func=AF.Copy, row0=0):
            for nb in range(4):
                bank = nbank()
                MM(bank[row0:row0 + nrows, :], [(bank[row0:row0 + nrows, :], w[:, k, cols[0]:cols[1]], SL[:, k, nb * 512:(nb + 1) * 512], k == 0, k == 7) for k in range(8)])
                ACT(dst[row0:row0 + nrows, nb * 512:(nb + 1) * 512], bank[row0:row0 + nrows, :], func)

        def wblock(wview, c0, ncols, bi):
            wb = SL[:, 23 + 2 * bi:25 + 2 * bi, :].rearrange("p a b -> p (a b)").rearrange("p (k n) -> p k n", n=512)
            wdma(wb[:, :, 0:ncols], wview[:, :, c0:c0 + ncols], "wst%d" % bi)
            return wb

        _pl = {"t": 0, "due": {}}

        def defer(delay, fn):
            _pl["due"].setdefault(_pl["t"] + delay, []).append(fn)

        def run_pipeline(items, stages):
            n = len(items)
            ml = max(l for _, l in stages)
            _pl["due"] = {}
            for t in range(n + ml):
                _pl["t"] = t
                for fn in _pl["due"].pop(t, []):
                    fn()
                for fn, lag in stages:
                    ii = t - lag
                    if 0 <= ii < n:
                        fn(ii, items[ii])
            t = n + ml
            while _pl["due"]:
                _pl["t"] = t
                for fn in _pl["due"].pop(t, []):
                    fn()
                t += 1

        ZB4 = [PB[0], PB[1], PB[4], PB[5]]

        ZB3 = [PB[0], PB[1], PB[7]]

        def sm_A(idx, qT, kT, rows, kt, q0, scale, bias, extra, zbs=None):
            d = kt - q0 // 128
            c0 = max(d, 0) * 128
            n = 512 - c0
            zbs = zbs or ZB4
            zb = zbs[idx % len(zbs)]
            mms = [(zb[:, c0:512], kT[rows[0]:rows[1], kt * 128:(kt + 1) * 128], qT[rows[0]:rows[1], q0 + c0:q0 + 512], True, len(extra) == 0)]
            for i, (cs, l, r) in enumerate(extra):
                mms.append((zb[:, cs:cs + 128], l, r, False, i == len(extra) - 1))
            MM(zb[:, c0:512], mms)
            P = SL[:, 22, (idx % 4) * 512:(idx % 4 + 1) * 512]
            ACT(P[:, 0:n], zb[:, c0:512], AF.Exp, scale=scale, bias=bias)
            return P, c0, n

        def RECIPA(out, in_, scratch):
            return S.op("dve", lambda e: e.reciprocal_approx_accurate(out=out, in_=in_, scratch=scratch), [out, scratch], [in_])

        def sm_B(P, c0, n, obank, dbank, v_lhsT, last, first=False):
            assert not first or (c0 == 0 and n == 512)
            MM(obank[:, c0:512], [(obank[:, c0:512], v_lhsT, P[:, 0:n], first, last)])
            if dbank is not None:
                MM(dbank[:, c0:512], [(dbank[:, c0:512], ones_b, P[:, 0:n], first, last)])

        def zinit(bank):
            MM(bank[:, :], [(bank[:, :], zero_b, SL[:, 0, 0:512], True, False)])

        HK = MW[:, 0:4096].rearrange("p (h a t) -> p h a t", a=2, t=128)
        SGP = SM[:, 78:79]
        prep_done = [False]

        def diff_prep(lam_init):
            if prep_done[0]:
                return
            prep_done[0] = True
            junk = SC[:, 6, 0:64]
            for i in range(2):
                S.op("dve", (lambda a, b, c: (lambda e: e.tensor_tensor(out=junk, in0=a, in1=b, op=ALU.mult)))(LAMV[:, 2 * i, :], LAMV[:, 2 * i + 1, :], None), [junk], [LAMV[:, 2 * i, :], LAMV[:, 2 * i + 1, :]])
                S.op("dve", (lambda o: (lambda e: e.reduce_sum(out=o, in_=junk, axis=mybir.AxisListType.X)))(LTMP[:, i:i + 1]), [LTMP[:, i:i + 1]], [junk])
            ACT(LTMP[:, 0:2], LTMP[:, 0:2], AF.Exp)
            TT("dve", NLAM, LTMP[:, 1:2], LTMP[:, 0:1], ALU.subtract)
            TS("dve", NLAM, NLAM, -lam_init, None, ALU.add)
            TS("dve", SGP, SG, 1.0 - lam_init, None, ALU.mult)
            MM(PB[6][:, 0:8], [(PB[6][:, 0:8], TB[:, 392:520], TB[:, 0:8], True, True)])
            CP("dve", CBIAS, PB[6][:, 0:8])
            MM(PB[7][0:8, 0:384], [(PB[7][0:8, 0:384], TB[:, 0:8], TB[:, 8:392], True, True)])
            gv = SC[0:8, 5, 0:384]
            CP("dve", gv, PB[7][0:8, 0:384])
            S.dma("sp", gscr, gv, "gscr", sb_out=False, sb_in=True)
            for h in range(8):
                hk = SC[:, 4, 0:256].rearrange("p (a t) -> p a t", t=128)
                for a in range(2):
                    src = bass.AP(gscr.tensor, h * 384 + a * 128, [[1, 128], [1, 128]])
                    o = S.dma("sp", hk[:, a, :], src, "hk")
                    o.deps.add(S.dkeys["gscr"][0])
                CP("dve", HK[:, h], hk)

        def layer_ab(l):
            norm_phase(l)
            gate_bc(l)
            w0v = win0_d.rearrange("(k p) n -> p k n", p=128)
            WUQ = MW[:, 0:2304].rearrange("p (k n) -> p k n", n=768)
            WUQR = MW[:, 2304:3072].rearrange("p (k n) -> p k n", n=256)
            WUKV = MW[:, 3072:5120].rearrange("p (k n) -> p k n", n=1024)
            WKR = MW[:, 5120:5376].rearrange("p (k n) -> p k n", n=32)
            wdma(WUQ, wuq_d.rearrange("(k p) n -> p k n", p=128), "wuq")
            wdma(WUKV, wukv_d.rearrange("(k p) n -> p k n", p=128), "wukv")
            wq4 = WUQ.rearrange("p k (h c) -> p k h c", c=96)
            wr4 = WUQR.rearrange("p k (h c) -> p k h c", c=32)
            for k in range(3):
                CP("pool", wr4[:, k, :, 0:16], wq4[:, k, :, 80:96])
                CP("pool", wr4[:, k, :, 16:32], wq4[:, k, :, 64:80])
            WG0 = SL[:, 23:27, :].rearrange("p a b -> p (a b)")[:, 0:8 * 672].rearrange("p (k n) -> p k n", n=672)
            wdma(WG0[:, :, 0:336], w0v[:, :, 0:336], "wst0")
            wdma(WG0[:, :, 336:672], w0v[:, :, 336:672], "wst1")
            for k in range(8):
                CP("pool", WKR[:, k, 0:16], WG0[:, k, 656:672])
                CP("pool", WKR[:, k, 16:32], WG0[:, k, 640:656])
            COS, SIN = sl(15), sl(16)
            CQ = [sl(8 + 3 + i) for i in range(3)]
            CKV = [sl(14), sl(17)]
            KT = sl(18)
            QT = sl(19)
            VA = sl(20).rearrange("p (t c) -> p t c", c=128)
            SZ = sl(21)

            def latent(dsts, col0, gcols, nfeat):
                nch = len(dsts)
                for nb in range(4):
                    ssb = PB[2 + nb % 2]
                    for c in range(nch):
                        bank = nbank()
                        MM(bank[:, :], [(bank[:, :], WG0[:, k, col0 + c * 128:col0 + (c + 1) * 128], SL[:, k, nb * 512:(nb + 1) * 512], k == 0, k == 7) for k in range(8)])
                        ACT(dsts[c][:, nb * 512:(nb + 1) * 512], bank[:, :], AF.Copy)
                        sq = SL[:, 22, (c % 2) * 512:(c % 2 + 1) * 512]
                        ACT(sq, bank[:, :], AF.Square)
                        MM(ssb[:, :], [(ssb[:, :], ones_b, sq, c == 0, c == nch - 1)])
                    r = sc(2)
                    TS("dve", r, ssb[:, :], 1.0 / nfeat, 1e-6, ALU.mult, ALU.add)
                    ACT(r, r, AF.Ln)
                    ACT(r, r, AF.Exp, scale=-0.5)
                    for c in range(nch):
                        STT("dve", dsts[c][:, nb * 512:(nb + 1) * 512], dsts[c][:, nb * 512:(nb + 1) * 512], gcols[:, c:c + 1], r, ALU.mult, ALU.mult)
            latent(CQ, 0, GQ, 384)
            latent(CKV, 384, GKV, 256)
            for nb in range(4):
                ba, bb = nbank(), nbank()
                MM(ba[0:96, :], [(ba[0:96, :], WG0[:, k, 576:672], SL[:, k, nb * 512:(nb + 1) * 512], k == 0, k == 7) for k in range(8)])
                MM(bb[64:96, :], [(bb[64:96, :], WKR[:, k, :], SL[:, k, nb * 512:(nb + 1) * 512], k == 0, k == 7) for k in range(8)])
                t1, t2 = SC[64:96, 2, :], SC[64:96, 3, :]
                TT("dve", t1, ba[64:96, :], COS[64:96, nb * 512:(nb + 1) * 512], ALU.mult)
                TT("dve", t2, bb[64:96, :], SIN[64:96, nb * 512:(nb + 1) * 512], ALU.mult)
                TT("pool", KT[64:96, nb * 512:(nb + 1) * 512], t1, t2, ALU.add)
            sc_mla = 96.0 ** -0.5
            wz_next = [wblock(w0v, 2208, 128, 0)]
            for h in range(8):
                par = h % 2
                ro = (0, 64) if par == 0 else (64, 128)
                rd = (64, 128) if par == 0 else (0, 64)
                if h % 4 == 3:
                    out_proj_load(wout0_d, h // 4)
                if par == 0:
                    wz = wz_next[0]
                    proj_fm(SZ, wz, (0, 128), 128, AF.Silu)
                    if h < 6:
                        wz_next[0] = wblock(w0v, 2208 + (h // 2 + 1) * 128, 128, (h // 2 + 1) % 2)
                for nb in range(4):
                    ba, bb = nbank(), nbank()
                    MM(ba[0:96, :], [(ba[0:96, :], WUQ[:, k, h * 96:(h + 1) * 96], CQ[k][:, nb * 512:(nb + 1) * 512], k == 0, k == 2) for k in range(3)])
                    MM(bb[64:96, :], [(bb[64:96, :], WUQR[:, k, h * 32:(h + 1) * 32], CQ[k][:, nb * 512:(nb + 1) * 512], k == 0, k == 2) for k in range(3)])
                    ACT(QT[0:64, nb * 512:(nb + 1) * 512], ba[0:64, :], AF.Copy)
                    t1, t2 = SC[64:96, 2, :], SC[64:96, 3, :]
                    TT("dve", t1, ba[64:96, :], COS[64:96, nb * 512:(nb + 1) * 512], ALU.mult)
                    TT("dve", t2, bb[64:96, :], SIN[64:96, nb * 512:(nb + 1) * 512], ALU.mult)
                    TT("pool", QT[64:96, nb * 512:(nb + 1) * 512], t1, t2, ALU.add)
                for nb in range(4):
                    bank = nbank()
                    MM(bank[0:64, :], [(bank[0:64, :], WUKV[:, k, h * 128:h * 128 + 64], CKV[k][:, nb * 512:(nb + 1) * 512], k == 0, k == 1) for k in range(2)])
                    ACT(KT[0:64, nb * 512:(nb + 1) * 512], bank[0:64, :], AF.Copy)
                MSET("pool", VA[:, :, rd[0]:rd[1]], 1.0)
                for half in range(2):
                    bank = nbank()
                    mms = []
                    for t8 in range(8):
                        tt = half * 8 + t8
                        for k in range(2):
                            mms.append((bank[:, t8 * 64:(t8 + 1) * 64], CKV[k][:, tt * 128:(tt + 1) * 128], WUKV[:, k, h * 128 + 64:h * 128 + 128], k == 0, k == 1))
                    MM(bank[:, :], mms)
                    CP("dve", VA[:, half * 8:(half + 1) * 8, ro[0]:ro[1]], bank[:, :].rearrange("p (t c) -> p t c", c=64))
                ych = 8 + (h // 2) % 2
                items = [(qb, kt) for qb in range(4) for kt in range(qb * 4 + 4)]
                stt = {}

                def mA(idx, it, h=h):
                    qb, kt = it
                    q0 = qb * 512
                    d = kt - q0 // 128
                    extra = [(d * 128, ident_b, negchk_b)] if d >= 0 else []
                    stt[idx] = sm_A(idx, QT, KT, (0, 96), kt, q0, sc_mla, 0.0, extra)

                def mB(idx, it, ro=ro, rd=rd, ych=ych):
                    qb, kt = it
                    q0 = qb * 512
                    ob = PB[2 + qb % 2]
                    P, c0, n = stt.pop(idx)
                    sm_B(P, c0, n, ob, None, VA[:, kt, :], kt == qb * 4 + 3, kt == 0)
                    if kt == qb * 4 + 3:
                        rc = SC[rd[0]:rd[1], 4, :]
                        RECIP(rc, ob[rd[0]:rd[1], :])
                        tq = SC[ro[0]:ro[1], 5, :]
                        TT("dve", tq, ob[ro[0]:ro[1], :], rc, ALU.mult)
                        TT("pool", sl(ych)[ro[0]:ro[1], q0:q0 + 512], tq, SZ[ro[0]:ro[1], q0:q0 + 512], ALU.mult)
                run_pipeline(items, [(mA, 0), (mB, 3)])
                if h % 4 == 3:
                    out_proj(wout0_d, h // 4, (8, 9))
            if list(layers) == [0, 1]:
                diff_prep(0.8 - 0.6 * math.exp(-0.3 * 1))
            QA, QB, KS, SZ2 = sl(11), sl(16), sl(12), sl(13)
            VS = SL[:, 14:16, :].rearrange("p a b -> p (a b)").rearrange("p (t h c) -> p t h c", h=2, c=128)
            MSET("pool", SL[:, 14:16, :], 0.0)
            MSET("pool", QA, 0.0)
            MSET("pool", QB, 0.0)
            sc_sb = 64.0 ** -0.5
            sbcols = lambda pr: (672 + pr * 128, 1184 + pr * 128, 1696 + pr * 128, 2208 + 512 + pr * 128)
            abuf = [SL[:, 17:19, :].rearrange("p a b -> p (a b)").rearrange("p (k n) -> p k n", n=512),
                    SL[:, 19:21, :].rearrange("p a b -> p (a b)").rearrange("p (k n) -> p k n", n=512)]
            if DEFER:
                mod_dma(1, 0, abuf[0], "ada0")
                mod_dma(1, 1, abuf[1], "ada1")
            wnext = load4(w0v, sbcols(0), 0) if PREFETCH else None
            for pr in range(4):
                wb = wnext if PREFETCH else load4(w0v, sbcols(pr), pr % 2)
                proj_fm2(QA, QB, wb, (0, 128))
                proj_fm(KS, wb, (128, 256), 128)
                proj_fm(SZ2, wb, (384, 512), 128, AF.Silu)
                for q4 in range(4):
                    bank = nbank()
                    mms = []
                    for t4 in range(4):
                        tt = q4 * 4 + t4
                        for k in range(8):
                            mms.append((bank[:, t4 * 128:(t4 + 1) * 128], SL[:, k, tt * 128:(tt + 1) * 128], wb[:, k, 256:384], k == 0, k == 7))
                    MM(bank[:, :], mms)
                    b3 = bank[:, :].rearrange("p (t c) -> p t c", c=128)
                    CP("dve", VS[:, q4 * 4:(q4 + 1) * 4, 0, 0:64], b3[:, :, 0:64])
                    CP("dve", VS[:, q4 * 4:(q4 + 1) * 4, 1, 64:128], b3[:, :, 64:128])
                if pr < 3 and PREFETCH:
                    wnext = load4(w0v, sbcols(pr + 1), (pr + 1) % 2)
                if pr % 2 == 1:
                    out_proj_load(wout0_d, 2 + pr // 2)
                ych = 8 + pr % 2
                items = []
                for qb in range(4):
                    for hh in range(2):
                        nk = qb * 4 + 4
                        for kt in range(nk - 1, -1, -1):
                            items.append((qb, hh, kt, kt == nk - 1, hh == 1 and kt == 0, qb * 2 + hh))
                S21F = SL[:, 21, :].bitcast(F32)
                EB = [sc(0), sc(1), sc(2), S21F[:, 0:512], S21F[:, 512:1024]]

                def geo(it):
                    qb, hh, kt, first, last, chain = it
                    q0 = qb * 512
                    d = kt - q0 // 128
                    c0 = max(d, 0) * 128
                    return q0, d, c0, 512 - c0

                def s_z(idx, it):
                    qb, hh, kt, first, last, chain = it
                    q0, d, c0, n = geo(it)
                    rows = (hh * 64, hh * 64 + 64)
                    if first:
                        MSET("pool", sc(5 + chain % 2), 0.0)
                    zb = PB[idx % 2]
                    mms = [(zb[:, c0:512], KS[:, kt * 128:(kt + 1) * 128], (QA if hh == 0 else QB)[:, q0 + c0:q0 + 512], True, d < 0)]
                    if d >= 0:
                        mms.append((zb[:, c0:c0 + 128], ident_b, negtri_b, False, True))
                    MM(zb[:, c0:512], mms)
                    ACT(EB[idx % 5][:, 0:n], zb[:, c0:512], AF.Exp, scale=sc_sb)

                def s_x(idx, it):
                    q0, d, c0, n = geo(it)
                    csf = sc(3 + idx % 2)
                    ACT(csf[:, 0:n], csf[:, 0:n], AF.Exp, scale=-1.0)
                    W = SL[:, 22, (idx % 2) * 512:(idx % 2) * 512 + 512]
                    TT("pool", W[:, 0:n], EB[idx % 5][:, 0:n], csf[:, 0:n], ALU.mult)

                def s_sp(idx, it):
                    q0, d, c0, n = geo(it)
                    SP = SL[:, 22, (2 + idx % 2) * 512:(2 + idx % 2) * 512 + 512]
                    ACT(SP[:, 0:n], EB[idx % 5][:, 0:n], AF.Ln, bias=1.0)

                def s_cs(idx, it):
                    qb, hh, kt, first, last, chain = it
                    q0, d, c0, n = geo(it)
                    carry = sc(5 + chain % 2)
                    SP = SL[:, 22, (2 + idx % 2) * 512:(2 + idx % 2) * 512 + 512]
                    cb, tb = PB[3 + idx % 2], PB[5 + 2 * (idx % 2)]
                    MM(cb[:, 0:n], [(cb[:, 0:n], U_b, SP[:, 0:n], True, True)])
                    MM(tb[:, 0:n], [(tb[:, 0:n], ones_b, SP[:, 0:n], True, True)])
                    if WARM_DUMMY:
                        MM(PB[6][:, :], [(PB[6][:, :], ones_b, SL[:, 0, 0:512], True, True)] * WARM_DUMMY)
                    csf = sc(3 + idx % 2)
                    TT("dve", csf[:, 0:n], cb[:, 0:n], carry[:, c0:512], ALU.add)
                    TT("dve", carry[:, c0:512], tb[:, 0:n], carry[:, c0:512], ALU.add)

                def s_pv(idx, it, ych=ych):
                    qb, hh, kt, first, last, chain = it
                    q0, d, c0, n = geo(it)
                    W = SL[:, 22, (idx % 2) * 512:(idx % 2) * 512 + 512]
                    ob = PB[2]
                    if first and hh == 0:
                        zinit(ob)
                    MM(ob[:, c0:512], [(ob[:, c0:512], VS[:, kt, hh, :], W[:, 0:n], False, last)])
                    if last:
                        TT("dve", sl(ych)[:, q0:q0 + 512], ob[:, :], SZ2[:, q0:q0 + 512], ALU.mult)
                run_pipeline(items, [(s_z, 0), (s_x, 2), (s_sp, 0), (s_cs, 1), (s_pv, 3)])
                if DEFER and pr < 3:
                    mod_mm(1, 2 * pr, abuf[0])
                    mod_mm(1, 2 * pr + 1, abuf[1])
                    if pr < 2:
                        mod_dma(1, 2 * pr + 2, abuf[0], "ada0")
                        mod_dma(1, 2 * pr + 3, abuf[1], "ada1")
                    else:
                        mod_fin(1)
                if pr % 2 == 1:
                    out_proj(wout0_d, 2 + pr // 2, (8, 9))

        def layer_diff(l, i_layer):
            lam_init = 0.8 - 0.6 * math.exp(-0.3 * i_layer)
            norm_phase(l)
            gate_bc(l)
            diff_prep(lam_init)
            w1v = win1_d.rearrange("(k p) n -> p k n", p=128)
            QD0, QD1, KD, SZ = sl(11), sl(15), sl(12), sl(14)
            VD = sl(13).rearrange("p (t c) -> p t c", c=128)
            MSET("pool", QD0, 0.0)
            MSET("pool", QD1, 0.0)
            sc_d = 64.0 ** -0.5
            dcols = lambda h: (h * 128, 1024 + h * 128, 2048 + h * 128, 3072 + h * 128)
            wnext = load4(w1v, dcols(0), 0) if PREFETCH else None
            for h in range(8):
                wb = wnext if PREFETCH else load4(w1v, dcols(h), h % 2)
                proj_fm2(QD0, QD1, wb, (0, 128))
                proj_fm(KD, wb, (128, 256), 128)
                proj_fm(SZ, wb, (384, 512), 128, AF.Silu)
                for q4 in range(4):
                    bank = nbank()
                    mms = []
                    for t4 in range(4):
                        tt = q4 * 4 + t4
                        for k in range(8):
                            mms.append((bank[:, t4 * 128:(t4 + 1) * 128], SL[:, k, tt * 128:(tt + 1) * 128], wb[:, k, 256:384], k == 0, k == 7))
                    MM(bank[:, :], mms)
                    CP("dve", VD[:, q4 * 4:(q4 + 1) * 4, :], bank[:, :].rearrange("p (t c) -> p t c", c=128))
                if h < 7 and PREFETCH:
                    wnext = load4(w1v, dcols(h + 1), (h + 1) % 2)
                if h % 2 == 1:
                    out_proj_load(wout1_d, h // 2)
                ych = 8 + h % 2
                obs = [PB[2], PB[3]]
                dbs = [PB[4], PB[5]]
                items = [(qb, c, kt) for qb in range(4) for c in range(2) for kt in range(qb * 4 + 4)]
                stt = {}

                def dA(idx, it, h=h):
                    qb, c, kt = it
                    q0 = qb * 512
                    d = kt - q0 // 128
                    extra = []
                    for e4 in range(max(d, 0), 4):
                        dl = d - e4
                        if dl == 0:
                            extra.append((e4 * 128, anti_b, HK[:, h, 0, :]))
                            extra.append((e4 * 128, ident_b, negchk_b))
                        elif dl == -1:
                            extra.append((e4 * 128, anti_b, HK[:, h, 1, :]))
                    stt[idx] = sm_A(idx, QD0 if c == 0 else QD1, KD, (0, 128), kt, q0, sc_d, 0.0, extra, ZB3)

                def dB(idx, it, ych=ych):
                    qb, c, kt = it
                    q0 = qb * 512
                    P, c0, n = stt.pop(idx)
                    last = kt == qb * 4 + 3
                    sm_B(P, c0, n, obs[c], dbs[c], VD[:, kt, :], last, kt == 0)
                    if c == 1 and last:
                        r0, r1, a0, a1 = sc(0), sc(1), sc(2), sc(3)
                        sq = SL[:, 21, 0:512]
                        yv = sl(ych)[:, q0:q0 + 512]
                        szv = SZ[:, q0:q0 + 512]
                        ACT(r0, dbs[0][:, :], AF.Ln)
                        CP("dve", a0, obs[0][:, :])
                        ACT(r1, dbs[1][:, :], AF.Ln)
                        CP("dve", a1, obs[1][:, :])

                        def e1():
                            ACT(r0, r0, AF.Exp, scale=-1.0)
                            ACT(r1, r1, AF.Exp, scale=-1.0)

                        def e2():
                            TT("dve", a0, a0, r0, ALU.mult)
                            TT("dve", a1, a1, r1, ALU.mult)
                            STT("dve", a0, a1, NLAM, a0, ALU.mult, ALU.add)

                        def e3():
                            ACT(sq, a0, AF.Square)
                            MM(PB[6][:, :], [(PB[6][:, :], ones_b, sq, True, True)])

                        def e4():
                            TS("dve", r0, PB[6][:, :], 1.0 / 128, 1e-5, ALU.mult, ALU.add)

                        def e5():
                            ACT(r0, r0, AF.Ln)

                        def e6():
                            ACT(r0, r0, AF.Exp, scale=-0.5)

                        def e7():
                            TT("dve", a0, a0, r0, ALU.mult)
                            STT("dve", yv, a0, SGP, szv, ALU.mult, ALU.mult)
                        defer(1, e1)
                        defer(2, e2)
                        defer(4, e3)
                        defer(5, e4)
                        defer(6, e5)
                        defer(7, e6)
                        defer(8, e7)
                run_pipeline(items, [(dA, 0), (dB, 3)])
                if h % 2 == 1:
                    out_proj(wout1_d, h // 2, (8, 9))

        for l in layers:
            if l % 2 == 0:
                layer_ab(l)
            else:
                layer_diff(l, l)
        if final:
            S.dma("sp", GBC[:], fg_d, "fg")
            SSQ = SM[:, 176:192]
            RSTD = SM[:, 160:176]
            MSET("pool", SSQ, 0.0)
            for tt in range(NT):
                ACT(sc2(0), X[:, tt, :], AF.Square, accum=SSQ[:, tt:tt + 1])
            TS("dve", SSQ, SSQ, 1.0 / D, 1e-6, ALU.mult, ALU.add)
            ACT(SSQ, SSQ, AF.Ln)
            ACT(RSTD, SSQ, AF.Exp, scale=-0.5)
            for tt in range(NT):
                STT("dve", X[:, tt, :], X[:, tt, :], RSTD[:, tt:tt + 1], GBC[:], ALU.mult, ALU.mult)
                S.dma("sp", out_d[tt * 128:(tt + 1) * 128, :], X[:, tt, :], "o%d" % (tt % 4), sb_out=False, sb_in=True)
        else:
            for tt in range(NT):
                S.dma("sp", out_d[tt * 128:(tt + 1) * 128, :], X[:, tt, :], "o%d" % (tt % 4), sb_out=False, sb_in=True)
        S.emit(st)
    return nc


_CACHE = {}
FUSED = True
DEFER_ADA = True
PREFETCH = True
WARM_DUMMY = 0


def _in_maps(inp, xs):
    cst = _consts()
    f = lambda a: np.ascontiguousarray(np.asarray(a, dtype=np.float32))
    maps = []
    for b in range(8):
        m = {
            "x": f(xs[b]),
            "c_t": f(np.asarray(inp["c"])[b].reshape(8, 128).T),
            "posi": np.full((128, 1), int(np.asarray(inp["pos_offset"])[b]), np.int32),
            "ada_w": f(inp["ada_w"]),
            "ada_b_t": f(np.asarray(inp["ada_b"]).reshape(2, 24, 128).transpose(2, 0, 1)),
            "norm_g_t": f(np.asarray(inp["norm_g"]).reshape(2, 8, 128).transpose(2, 0, 1)),
            "final_g_bc": f(np.broadcast_to(np.asarray(inp["final_g"])[None, :], (128, D))),
            "ab_w_in": f(np.asarray(inp["ab_w_in"])[0]),
            "gq_t": f(np.asarray(inp["ab_q_norm_g"])[0].reshape(3, 128).T),
            "gkv_t": f(np.asarray(inp["ab_kv_norm_g"])[0].reshape(2, 128).T),
            "ab_w_uq": f(np.asarray(inp["ab_w_uq"])[0]),
            "ab_w_ukv": f(np.asarray(inp["ab_w_ukv"])[0]),
            "ab_w_out": f(np.asarray(inp["ab_w_out"])[0]),
            "dif_w_in": f(np.asarray(inp["dif_w_in"])[0]),
            "lamv": f(np.broadcast_to(np.stack([np.asarray(inp[k])[0] for k in ("dif_lam_q1", "dif_lam_k1", "dif_lam_q2", "dif_lam_k2")])[None], (128, 4, 64))),
            "subln_t": f(np.asarray(inp["dif_subln_g"])[0].reshape(128, 1)),
            "dif_w_out": f(np.asarray(inp["dif_w_out"])[0]),
            "rel_tab": f(inp["rel_bias_table"]),
        }
        m.update(cst)
        maps.append(m)
    return maps


def _run(layers, final, inp, xs):
    key = (tuple(layers), final)
    if key not in _CACHE:
        _CACHE[key] = build(list(layers), final)
    res = run_bass_kernel_spmd(_CACHE[key], _in_maps(inp, xs), core_ids=list(range(8)))
    return np.stack([r["out"] for r in res.results], 0)


def kernel(**inp):
    x = np.asarray(inp["x"], dtype=np.float32)
    if FUSED:
        return _run((0, 1), True, inp, x)
    x1 = _run((0,), False, inp, x)
    return _run((1,), True, inp, x1)
```

```python
import math
import numpy as np
from contextlib import ExitStack
import concourse.bass as bass
import concourse.mybir as mybir
from concourse.bass_utils import run_bass_kernel_spmd

F32 = mybir.dt.float32
BF16 = mybir.dt.bfloat16
I32 = mybir.dt.int32
AF = mybir.ActivationFunctionType
ALU = mybir.AluOpType

_DTS = {F32: 4, BF16: 2, I32: 4}


def _region(ap):
    pat = ap.ap
    ds = _DTS.get(ap.dtype, 4)
    pstride = pat[0][0]
    off = ap.offset
    if pstride == 0:
        p0, f0, np_ = 0, off, 1
    else:
        p0, f0, np_ = off // pstride, off % pstride, pat[0][1]
    lo = hi = f0
    for st, cnt in pat[1:]:
        if st >= 0:
            hi += st * (cnt - 1)
        else:
            lo += st * (cnt - 1)
    if ap.tensor.name.startswith("PB"):
        return (ap.tensor.name, 0, 128, 0, 2048)
    return (ap.tensor.name, p0, p0 + np_, lo * ds, (hi + 1) * ds)


class Op:
    __slots__ = ("eng", "fn", "deps", "idx", "sig", "signum", "dma", "dsem", "dval")


class Sched:
    ENG = ("pe", "act", "dve", "pool", "sp")

    def __init__(self, nc):
        self.nc = nc
        self.ops = []
        self.rec = {}
        self.dkeys = {}

    def _deps_for(self, idx, eng, is_dma, reads, writes):
        deps = set()
        for ap in reads:
            name, p0, p1, b0, b1 = _region(ap)
            lst = self.rec.setdefault(name, [])
            psum = name.startswith("PB")
            for r in lst:
                if r[5] and r[0] < p1 and p0 < r[1] and r[2] < b1 and b0 < r[3]:
                    deps.add(r[4])
                elif psum and not r[5] and self.ops[r[4]].eng != eng:
                    deps.add(r[4])
            key = (p0, p1, b0, b1)
            for r in lst:
                if (not r[5]) and (r[0], r[1], r[2], r[3]) == key:
                    o = self.ops[r[4]]
                    if o.eng == eng and not o.dma and not is_dma:
                        r[4] = idx
                        break
            else:
                lst.append([p0, p1, b0, b1, idx, False])
        for ap in writes:
            name, p0, p1, b0, b1 = _region(ap)
            lst = self.rec.setdefault(name, [])
            keep = []
            for r in lst:
                if r[0] < p1 and p0 < r[1] and r[2] < b1 and b0 < r[3]:
                    if r[4] != idx:
                        deps.add(r[4])
                    if r[0] >= p0 and r[1] <= p1 and r[2] >= b0 and r[3] <= b1 and r[4] != idx:
                        continue
                keep.append(r)
            keep.append([p0, p1, b0, b1, idx, True])
            self.rec[name] = keep
        deps.discard(idx)
        return deps

    def op(self, eng, fn, outs, ins):
        o = Op()
        o.eng, o.fn, o.idx, o.dma, o.sig, o.signum = eng, fn, len(self.ops), False, False, 0
        o.dsem = o.dval = None
        self.ops.append(o)
        o.deps = self._deps_for(o.idx, eng, False, ins, outs)
        return o

    def dma(self, queue, out, in_, key, sb_out=True, sb_in=False):
        o = Op()
        o.eng, o.idx, o.dma, o.sig, o.signum = queue, len(self.ops), True, False, 0
        o.fn = lambda e: e.dma_start(out=out, in_=in_)
        self.ops.append(o)
        o.deps = self._deps_for(o.idx, queue, True, [in_] if sb_in else [], [out] if sb_out else [])
        st = self.dkeys.setdefault(key, [None, 0])
        if st[0] is not None:
            o.deps.add(st[0])
        st[0] = o.idx
        st[1] += 16
        o.dsem, o.dval = key, st[1]
        return o

    def emit(self, stack):
        nc = self.nc
        engs = {"pe": nc.tensor, "act": nc.scalar, "dve": nc.vector, "pool": nc.gpsimd, "sp": nc.sync}
        ops = self.ops
        known = {e: {} for e in self.ENG}
        vcs = [None] * len(ops)
        plan = []
        self.nwaits = 0

        def merge(kn, vc):
            for k, v in vc.items():
                if kn.get(k, -1) < v:
                    kn[k] = v
        for o in ops:
            waits_e = {}
            waits_d = {}
            for d in o.deps:
                p = ops[d]
                if p.dma:
                    if waits_d.get(p.dsem, (0, -1))[0] < p.dval:
                        waits_d[p.dsem] = (p.dval, d)
                else:
                    if p.eng == o.eng and not o.dma and o.eng == "pe":
                        continue
                    if waits_e.get(p.eng, -1) < d:
                        waits_e[p.eng] = d
            kn = known[o.eng]
            we = {}
            for e, d in sorted(waits_e.items(), key=lambda x: -x[1]):
                if kn.get(("e", e), -1) >= d:
                    continue
                kn[("e", e)] = d
                merge(kn, vcs[d])
                ops[d].sig = True
                we[e] = d
            wd = {}
            for k, (v, d) in waits_d.items():
                if kn.get(("d", k), 0) >= v:
                    continue
                kn[("d", k)] = v
                merge(kn, vcs[d])
                wd[k] = v
            self.nwaits += len(we) + len(wd)
            vc = dict(kn)
            if o.dma:
                vc[("d", o.dsem)] = o.dval
            else:
                vc[("e", o.eng)] = max(vc.get(("e", o.eng), -1), o.idx)
            vcs[o.idx] = vc
            plan.append((we, wd))
        cnt = {e: 0 for e in self.ENG}
        for o in ops:
            if o.sig and not o.dma:
                cnt[o.eng] += 1
                o.signum = cnt[o.eng]
        esem = {e: stack.enter_context(nc.semaphore("s_" + e)) for e in self.ENG}
        dsem = {k: stack.enter_context(nc.semaphore("d_" + str(k))) for k in self.dkeys}
        for o, (we, wd) in zip(ops, plan):
            eh = engs[o.eng]
            for e, d in we.items():
                eh.wait_ge(esem[e], ops[d].signum)
            for k, v in wd.items():
                eh.wait_ge(dsem[k], v)
            ins = o.fn(eh)
            if o.dma:
                ins.then_inc(dsem[o.dsem], 16)
            elif o.sig:
                ins.then_inc(esem[o.eng], 1)
        for k, st in self.dkeys.items():
            nc.sync.wait_ge(dsem[k], st[1])
        return cnt


D = 1024
S_ = 2048
NT = 16
NEG = -30000.0


def _t5_bucket(rel):
    nb, me = 16, 8
    ret = np.where(rel > 0, nb, 0)
    n = np.abs(rel)
    nf = np.maximum(n, 1).astype(np.float32)
    large = me + (np.log(nf / me) / math.log(128 / me) * (nb - me)).astype(np.int32)
    large = np.minimum(large, nb - 1)
    return ret + np.where(n < me, n, large)


def _consts():
    c = {}
    j = np.arange(128)[:, None]
    t = np.arange(128)[None, :]
    m = np.zeros((128, 8, 128), np.float32)
    m[:, 0] = np.eye(128)
    m[:, 1] = (j == 127 - t)
    m[:, 2] = (j >= t)
    m[:, 3] = 1.0
    m[:, 4] = 0.0
    m[:, 5] = np.where(j < t, 0.0, NEG)
    m[:, 6] = np.where((j // 64) <= (t // 64), 0.0, NEG)
    c["cmat"] = m.reshape(128, 1024)
    cc = np.zeros((128, 8), np.float32)
    p = np.arange(128)
    cc[:, 0] = 10000.0 ** (-(p % 16) / 16.0)
    cc[:, 1] = np.where((p % 32) < 16, -1.0, 1.0)
    c["ccol"] = cc
    oh = np.zeros((32, 384), np.float32)
    i = np.arange(383)
    b = _t5_bucket(127 - i)
    oh[b, i] = 1.0
    oh[15, :383] -= 1.0
    oh *= 8.0
    c["oh1d"] = oh
    sel = np.zeros((32, 128), np.float32)
    sel[15, :] = 1.0
    c["sel15"] = sel
    c["trow"] = np.arange(S_, dtype=np.float32)[None, :].repeat(128, 0)
    return c


def build(layers, final):
    nc = bass.Bass("TRN2", target_bir_lowering=False)
    dt = lambda n, s, d=F32, k="ExternalInput": nc.dram_tensor(n, s, d, kind=k).ap()
    x_d = dt("x", [S_, D])
    out_d = dt("out", [S_, D], F32, "ExternalOutput")
    ct_d = dt("c_t", [128, 8])
    pos_d = dt("posi", [128, 1], I32)
    adaw_d = dt("ada_w", [2, D, 3 * D])
    adab_d = dt("ada_b_t", [128, 2, 24])
    ng_d = dt("norm_g_t", [128, 2, 8])
    fg_d = dt("final_g_bc", [128, D])
    win0_d = dt("ab_w_in", [D, 3232])
    gq_d = dt("gq_t", [128, 3])
    gkv_d = dt("gkv_t", [128, 2])
    wuq_d = dt("ab_w_uq", [384, 768])
    wukv_d = dt("ab_w_ukv", [256, 1024])
    wout0_d = dt("ab_w_out", [D, D])
    win1_d = dt("dif_w_in", [D, 4096])
    lamv_d = dt("lamv", [128, 4, 64])
    sg_d = dt("subln_t", [128, 1])
    wout1_d = dt("dif_w_out", [D, D])
    tab_d = dt("rel_tab", [32, 8])
    cmat_d = dt("cmat", [128, 1024])
    ccol_d = dt("ccol", [128, 8])
    oh_d = dt("oh1d", [32, 384])
    sel_d = dt("sel15", [32, 128])
    trow_d = dt("trow", [128, S_])
    gscr = nc.dram_tensor("gscr", [8, 384], F32, kind="Internal").ap()

    st = ExitStack()
    with st:
        sbt = lambda n, s, d: st.enter_context(nc.sbuf_tensor(n, s, d))
        X = sbt("X", [128, NT, D], F32)
        NSL = 27
        SL = sbt("SL", [128, NSL, 2048], BF16)
        MW = sbt("MW", [128, 5376], BF16)
        GBC = sbt("GBC", [128, D], F32)
        NSC = 7
        SC = sbt("SC", [128, NSC, 512], F32)
        CB = sbt("CB", [128, 8, 128], BF16)
        CF = sbt("CF", [128, 2, 128], F32)
        CC = sbt("CC", [128, 8], F32)
        SM = sbt("SM", [128, 192], F32)
        SMB = sbt("SMB", [128, 8], BF16)
        TB = sbt("TB", [32, 8 + 384 + 128], F32)
        POSI = sbt("POSI", [128, 1], I32)
        LAMV = sbt("LAMV", [128, 4, 64], F32)
        PB = [st.enter_context(nc.psum_tensor("PB%d" % i, [128, 512], F32)) for i in range(8)]

        S = Sched(nc)
        sl = lambda i: SL[:, i, :]
        ident_b, anti_b, U_b, ones_b, zero_b, negtri_b, negchk_b = [CB[:, i, :] for i in range(7)]
        ident_f, ones_f = CF[:, 0, :], CF[:, 1, :]

        def ACT(out, in_, func, scale=1.0, bias=0.0, accum=None, extra_in=()):
            kw = {}
            if accum is not None:
                kw["accum_out"] = accum
            ins = [in_] + list(extra_in)
            if not isinstance(scale, (int, float)):
                ins.append(scale)
            if not isinstance(bias, (int, float)):
                ins.append(bias)
            outs = [out] + ([accum] if accum is not None else [])
            if not (isinstance(scale, (int, float)) and scale == 1.0):
                kw["scale"] = scale
            if not (isinstance(bias, (int, float)) and bias == 0.0):
                kw["bias"] = bias
            return S.op("act", lambda e: e.activation(out=out, in_=in_, func=func, **kw), outs, ins)

        def TT(eng, out, in0, in1, op):
            return S.op(eng, lambda e: e.tensor_tensor(out=out, in0=in0, in1=in1, op=op), [out], [in0, in1])

        def TS(eng, out, in0, s1, s2, op0, op1=None):
            ins = [in0] + [s for s in (s1, s2) if s is not None and not isinstance(s, (int, float))]
            if op1 is None:
                return S.op(eng, lambda e: e.tensor_scalar(out=out, in0=in0, scalar1=s1, scalar2=None, op0=op0), [out], ins)
            return S.op(eng, lambda e: e.tensor_scalar(out=out, in0=in0, scalar1=s1, scalar2=s2, op0=op0, op1=op1), [out], ins)

        def STT(eng, out, in0, sc, in1, op0, op1):
            ins = [in0, in1] + ([sc] if not isinstance(sc, (int, float)) else [])
            return S.op(eng, lambda e: e.scalar_tensor_tensor(out=out, in0=in0, scalar=sc, in1=in1, op0=op0, op1=op1), [out], ins)

        def CP(eng, out, in_):
            return S.op(eng, lambda e: e.tensor_copy(out=out, in_=in_), [out], [in_])

        def MSET(eng, out, v):
            return S.op(eng, lambda e: e.memset(out, v), [out], [])

        def RECIP(out, in_):
            return S.op("dve", lambda e: e.reciprocal(out=out, in_=in_), [out], [in_])

        def MM(outreg, mms):
            ins = []
            for (_, l, r, _, _) in mms:
                ins += [l, r]

            def fn(e):
                last = None
                for (o, l, r, a, b) in mms:
                    last = e.matmul(o, lhsT=l, rhs=r, start=a, stop=b)
                return last
            return S.op("pe", fn, [outreg], ins)

        def TR(outreg, trs):
            ins = []
            for (_, i, idn) in trs:
                ins += [i, idn]

            def fn(e):
                last = None
                for (o, i, idn) in trs:
                    last = e.transpose(o, i, idn)
                return last
            return S.op("pe", fn, [outreg], ins)

        dq = [0]

        def wdma(out, in_, key):
            return S.dma("pool", out, in_, key)

        def sc(i, w=512):
            return SC[:, i, 0:w]

        def sc2(i):
            return SC[:, i:i + 2, :].rearrange("p a b -> p (a b)")

        CST = SC[:, 0:2, :].rearrange("p a b -> p (a b)")
        S.dma("sp", CST, cmat_d, "c0")
        S.dma("sp", CC[:], ccol_d, "c1")
        S.dma("sp", TB[:, 0:8], tab_d, "c2")
        S.dma("sp", TB[:, 8:392], oh_d, "c2")
        S.dma("sp", TB[:, 392:520], sel_d, "c2")
        S.dma("sp", POSI[:], pos_d, "c1")
        S.dma("sp", LAMV[:], lamv_d, "c1")
        S.dma("sp", SM[:, 0:8], ct_d, "c3")
        S.dma("sp", SM[:, 8:56].rearrange("p (a b) -> p a b", b=24), adab_d, "c3")
        S.dma("sp", SM[:, 56:72].rearrange("p (a b) -> p a b", b=8), ng_d, "c3")
        S.dma("sp", SM[:, 72:75], gq_d, "c3")
        S.dma("sp", SM[:, 75:77], gkv_d, "c3")
        S.dma("sp", SM[:, 77:78], sg_d, "c3")
        CT, ADAB, NG, GQ, GKV, SG = SM[:, 0:8], SM[:, 8:56], SM[:, 56:72], SM[:, 72:75], SM[:, 75:77], SM[:, 77:78]
        MODC = SM[:, 80:128]
        GMOD = SM[:, 128:144]
        POSF = SM[:, 144:145]
        NLAM = SM[:, 145:146]
        LTMP = SM[:, 146:150]
        CBIAS = SM[:, 150:158]
        CP("dve", CB[:].rearrange("p a b -> p (a b)"), CST)
        CP("dve", CF[:, 0, :], CST[:, 0:128])
        CP("dve", CF[:, 1, :], CST[:, 384:512])
        if 0 in layers:
            COS, SIN = sl(15), sl(16)
            MSET("pool", SM[:, 159:160], math.pi / 2)
            CP("dve", POSF, POSI[:])
            for nb in range(4):
                ang = sc(0)
                S.dma("sp", ang, trow_d[:, nb * 512:(nb + 1) * 512], "trow")
                TS("dve", ang, ang, POSF, CC[:, 0:1], ALU.add, ALU.mult)
                kf, ki = sc(1), sc(3).bitcast(I32)
                TS("dve", kf, ang, 1.0 / (2 * math.pi), None, ALU.mult)
                CP("dve", ki, kf)
                CP("dve", kf, ki)
                STT("dve", ang, kf, -2.0 * math.pi, ang, ALU.mult, ALU.add)
                s2, c2 = sc(1), sc(2)
                ACT(s2, ang, AF.Sin, scale=0.5)
                ACT(c2, ang, AF.Sin, scale=-0.5, bias=SM[:, 159:160])
                STT("dve", c2, s2, 2.0, c2, ALU.mult, ALU.mult)
                TS("dve", SIN[:, nb * 512:(nb + 1) * 512], c2, CC[:, 1:2], None, ALU.mult)
                TT("dve", s2, s2, s2, ALU.mult)
                TS("dve", COS[:, nb * 512:(nb + 1) * 512], s2, -2.0, 1.0, ALU.mult, ALU.add)

        ada_stg = {}
        _adv0 = adaw_d.rearrange("l (k p) n -> l p k n", p=128)
        for nb, s0 in ((2, 8), (3, 17)):
            ada_stg[nb] = SL[:, s0:s0 + 4, :].rearrange("p a b -> p (a b)").bitcast(F32).rearrange("p (k n) -> p k n", n=512)
            S.dma("sp", ada_stg[nb], _adv0[layers[0], :, :, nb * 512:(nb + 1) * 512], "adf%d" % nb)
        for tt in range(NT):
            S.dma("sp", X[:, tt, :], x_d[tt * 128:(tt + 1) * 128, :], "x%d" % (tt % 8))

        ACT(SMB[:, 0:8], CT, AF.Silu)
        CACT = SMB[:, 0:8]
        adv = adaw_d.rearrange("l (k p) n -> l p k n", p=128)
        DEFER = DEFER_ADA and (list(layers) == [0, 1])

        def mod_dma(l, nb, wb, key):
            wdma(wb, adv[l, :, :, nb * 512:(nb + 1) * 512], key)

        def mod_mm(l, nb, wb):
            mms = []
            for jj in range(4):
                col = PB[6][:, nb * 4 + jj:nb * 4 + jj + 1]
                for k in range(8):
                    mms.append((col, wb[:, k, jj * 128:(jj + 1) * 128], CACT[:, k:k + 1], k == 0, k == 7))
            MM(PB[6][:, 0:32], mms)
            o_ = MODC[:, l * 24 + nb * 4:l * 24 + nb * 4 + 4]
            i0_, i1_ = PB[6][:, nb * 4:nb * 4 + 4], ADAB[:, l * 24 + nb * 4:l * 24 + nb * 4 + 4]
            S.op("dve", lambda e: e.tensor_tensor(out=o_, in0=i0_, in1=i1_, op=ALU.add), [o_], [PB[6][:, 0:32], i1_])

        def mod_fin(l):
            TS("dve", GMOD[:, l * 8:(l + 1) * 8], MODC[:, l * 24 + 8:l * 24 + 16], 1.0, None, ALU.add)
            TT("dve", GMOD[:, l * 8:(l + 1) * 8], GMOD[:, l * 8:(l + 1) * 8], NG[:, l * 8:(l + 1) * 8], ALU.mult)

        def stream_buf(bi):
            return SL[:, 23 + 2 * bi:25 + 2 * bi, :].rearrange("p a b -> p (a b)").rearrange("p (k n) -> p k n", n=512)

        blk_i = [0]
        for li, l in enumerate(layers[:1] if DEFER else layers):
            if li == 0:
                pb4 = [SL[:, 2 * q:2 * q + 2, :].rearrange("p a b -> p (a b)").rearrange("p (k n) -> p k n", n=512) for q in range(4)]
                for nb in range(2):
                    mod_dma(l, nb, pb4[nb], "adp%d" % nb)
                for nb in (2, 3):
                    CP("dve", pb4[nb], ada_stg[nb])
                for nb in (4, 5):
                    mod_dma(l, nb, stream_buf(nb % 2), "wst%d" % (nb % 2))
                for nb in range(6):
                    mod_mm(l, nb, pb4[nb] if nb < 4 else stream_buf(nb % 2))
            else:
                for nb in range(6):
                    bi = blk_i[0] % 2
                    blk_i[0] += 1
                    mod_dma(l, nb, stream_buf(bi), "wst%d" % bi)
                    mod_mm(l, nb, stream_buf(bi))
            mod_fin(l)


        def norm_phase(l):
            SS = SM[:, 158:159]
            RSTD = SM[:, 160:176]
            SSQ = SM[:, 176:192]
            MSET("pool", SSQ, 0.0)
            for tt in range(NT):
                ACT(sc2(0), X[:, tt, :], AF.Square, accum=SSQ[:, tt:tt + 1])
            TS("dve", SSQ, SSQ, 1.0 / D, 1e-6, ALU.mult, ALU.add)
            ACT(SSQ, SSQ, AF.Ln)
            ACT(RSTD, SSQ, AF.Exp, scale=-0.5)
            def xs_of(tt):
                return sc2(2 * (tt % 2))

            ACT(xs_of(0), X[:, 0, :], AF.Identity, scale=RSTD[:, 0:1])
            for tt in range(NT):
                xs = xs_of(tt)
                if tt + 1 < NT:
                    ACT(xs_of(tt + 1), X[:, tt + 1, :], AF.Identity, scale=RSTD[:, tt + 1:tt + 2])
                for half in range(2):
                    bank = PB[(tt * 2 + half) % 4]
                    TR(bank[:, :], [(bank[:, c * 128:(c + 1) * 128], xs[:, (half * 4 + c) * 128:(half * 4 + c + 1) * 128], ident_f) for c in range(4)])
                    for c in range(4):
                        cc = half * 4 + c
                        if half == 0:
                            TS("dve", SL[:, cc, tt * 128:(tt + 1) * 128], bank[:, c * 128:(c + 1) * 128],
                               GMOD[:, l * 8 + cc:l * 8 + cc + 1], MODC[:, l * 24 + cc:l * 24 + cc + 1], ALU.mult, ALU.add)
                        else:
                            ACT(SL[:, cc, tt * 128:(tt + 1) * 128], bank[:, c * 128:(c + 1) * 128], AF.Identity,
                                scale=GMOD[:, l * 8 + cc:l * 8 + cc + 1], bias=MODC[:, l * 24 + cc:l * 24 + cc + 1])

        def gate_bc(l):
            for c in range(8):
                dg = SC[:, 6, 0:128]
                TS("dve", dg, ident_f, MODC[:, l * 24 + 16 + c:l * 24 + 17 + c], None, ALU.mult)
                MM(PB[7][:, (c % 4) * 128:(c % 4 + 1) * 128], [(PB[7][:, (c % 4) * 128:(c % 4 + 1) * 128], ones_f, dg, True, True)])
                CP("dve", GBC[:, c * 128:(c + 1) * 128], PB[7][:, (c % 4) * 128:(c % 4 + 1) * 128])

        _rot = [0]
        ROT = [6, 7, 0, 1, 4, 5]

        def nbank():
            b = PB[ROT[_rot[0] % len(ROT)]]
            _rot[0] += 1
            return b

        def out_proj_load(wout_d, g):
            wo = SL[:, 10, :].rearrange("p (c n) -> p c n", n=1024)
            wv = wout_d.rearrange("(c p) n -> p c n", p=128)
            wdma(wo, wv[:, 2 * g:2 * g + 2, :], "wout")
            for c in range(2):
                TT("dve", wo[:, c, :], wo[:, c, :], GBC[:], ALU.mult)

        def out_proj(wout_d, g, ychunks):
            wo = SL[:, 10, :].rearrange("p (c n) -> p c n", n=1024)
            for tt in range(NT):
                for nb in range(2):
                    bank = PB[[6, 7, 0, 1][(tt * 2 + nb) % 4]]
                    MM(bank[:, :], [(bank[:, :], sl(ychunks[c])[:, tt * 128:(tt + 1) * 128], wo[:, c, nb * 512:(nb + 1) * 512], c == 0, c == 1) for c in range(2)])
                    xv = X[:, tt, nb * 512:(nb + 1) * 512]
                    ti = tt * 2 + nb
                    if ti % 3 == 2:
                        tmp = sc(ti // 3 % 2)
                        ACT(tmp, bank[:, :], AF.Copy)
                        TT("pool", xv, xv, tmp, ALU.add)
                    else:
                        TT("dve", xv, xv, bank[:, :], ALU.add)

        def proj_fm2(dA, dB, w, cols):
            for nb in range(4):
                bank = nbank()
                MM(bank[:, :], [(bank[:, :], w[:, k, cols[0]:cols[1]], SL[:, k, nb * 512:(nb + 1) * 512], k == 0, k == 7) for k in range(8)])
                ACT(dA[0:64, nb * 512:(nb + 1) * 512], bank[0:64, :], AF.Copy)
                ACT(dB[64:128, nb * 512:(nb + 1) * 512], bank[64:128, :], AF.Copy)

        def load4(wview, cols4, bi):
            wb = stream_buf(bi)
            for i4, c0 in enumerate(cols4):
                wdma(wb[:, :, i4 * 128:(i4 + 1) * 128], wview[:, :, c0:c0 + 128], "wst%d" % bi)
            return wb

        def proj_fm(dst, w, cols, nrows, func=AF.Copy, row0=0):
            for nb in range(4):
                bank = nbank()
                MM(bank[row0:row0 + nrows, :], [(bank[row0:row0 + nrows, :], w[:, k, cols[0]:cols[1]], SL[:, k, nb * 512:(nb + 1) * 512], k == 0, k == 7) for k in range(8)])
                ACT(dst[row0:row0 + nrows, nb * 512:(nb + 1) * 512], bank[row0:row0 + nrows, :], func)

        def wblock(wview, c0, ncols, bi):
            wb = SL[:, 23 + 2 * bi:25 + 2 * bi, :].rearrange("p a b -> p (a b)").rearrange("p (k n) -> p k n", n=512)
            wdma(wb[:, :, 0:ncols], wview[:, :, c0:c0 + ncols], "wst%d" % bi)
            return wb

        _pl = {"t": 0, "due": {}}

        def defer(delay, fn):
            _pl["due"].setdefault(_pl["t"] + delay, []).append(fn)

        def run_pipeline(items, stages):
            n = len(items)
            ml = max(l for _, l in stages)
            _pl["due"] = {}
            for t in range(n + ml):
                _pl["t"] = t
                for fn in _pl["due"].pop(t, []):
                    fn()
                for fn, lag in stages:
                    ii = t - lag
                    if 0 <= ii < n:
                        fn(ii, items[ii])
            t = n + ml
            while _pl["due"]:
                _pl["t"] = t
                for fn in _pl["due"].pop(t, []):
                    fn()
                t += 1

        ZB4 = [PB[0], PB[1], PB[4], PB[5]]

        ZB3 = [PB[0], PB[1], PB[7]]

        def sm_A(idx, qT, kT, rows, kt, q0, scale, bias, extra, zbs=None):
            d = kt - q0 // 128
            c0 = max(d, 0) * 128
            n = 512 - c0
            zbs = zbs or ZB4
            zb = zbs[idx % len(zbs)]
            mms = [(zb[:, c0:512], kT[rows[0]:rows[1], kt * 128:(kt + 1) * 128], qT[rows[0]:rows[1], q0 + c0:q0 + 512], True, len(extra) == 0)]
            for i, (cs, l, r) in enumerate(extra):
                mms.append((zb[:, cs:cs + 128], l, r, False, i == len(extra) - 1))
            MM(zb[:, c0:512], mms)
            P = SL[:, 22, (idx % 4) * 512:(idx % 4 + 1) * 512]
            ACT(P[:, 0:n], zb[:, c0:512], AF.Exp, scale=scale, bias=bias)
            return P, c0, n

        def RECIPA(out, in_, scratch):
            return S.op("dve", lambda e: e.reciprocal_approx_accurate(out=out, in_=in_, scratch=scratch), [out, scratch], [in_])

        def sm_B(P, c0, n, obank, dbank, v_lhsT, last, first=False):
            assert not first or (c0 == 0 and n == 512)
            MM(obank[:, c0:512], [(obank[:, c0:512], v_lhsT, P[:, 0:n], first, last)])
            if dbank is not None:
                MM(dbank[:, c0:512], [(dbank[:, c0:512], ones_b, P[:, 0:n], first, last)])

        def zinit(bank):
            MM(bank[:, :], [(bank[:, :], zero_b, SL[:, 0, 0:512], True, False)])

        HK = MW[:, 0:4096].rearrange("p (h a t) -> p h a t", a=2, t=128)
        SGP = SM[:, 78:79]
        prep_done = [False]

        def diff_prep(lam_init):
            if prep_done[0]:
                return
            prep_done[0] = True
            junk = SC[:, 6, 0:64]
            for i in range(2):
                S.op("dve", (lambda a, b, c: (lambda e: e.tensor_tensor(out=junk, in0=a, in1=b, op=ALU.mult)))(LAMV[:, 2 * i, :], LAMV[:, 2 * i + 1, :], None), [junk], [LAMV[:, 2 * i, :], LAMV[:, 2 * i + 1, :]])
                S.op("dve", (lambda o: (lambda e: e.reduce_sum(out=o, in_=junk, axis=mybir.AxisListType.X)))(LTMP[:, i:i + 1]), [LTMP[:, i:i + 1]], [junk])
            ACT(LTMP[:, 0:2], LTMP[:, 0:2], AF.Exp)
            TT("dve", NLAM, LTMP[:, 1:2], LTMP[:, 0:1], ALU.subtract)
            TS("dve", NLAM, NLAM, -lam_init, None, ALU.add)
            TS("dve", SGP, SG, 1.0 - lam_init, None, ALU.mult)
            MM(PB[6][:, 0:8], [(PB[6][:, 0:8], TB[:, 392:520], TB[:, 0:8], True, True)])
            CP("dve", CBIAS, PB[6][:, 0:8])
            MM(PB[7][0:8, 0:384], [(PB[7][0:8, 0:384], TB[:, 0:8], TB[:, 8:392], True, True)])
            gv = SC[0:8, 5, 0:384]
            CP("dve", gv, PB[7][0:8, 0:384])
            S.dma("sp", gscr, gv, "gscr", sb_out=False, sb_in=True)
            for h in range(8):
                hk = SC[:, 4, 0:256].rearrange("p (a t) -> p a t", t=128)
                for a in range(2):
                    src = bass.AP(gscr.tensor, h * 384 + a * 128, [[1, 128], [1, 128]])
                    o = S.dma("sp", hk[:, a, :], src, "hk")
                    o.deps.add(S.dkeys["gscr"][0])
                CP("dve", HK[:, h], hk)

        def layer_ab(l):
            norm_phase(l)
            gate_bc(l)
            w0v = win0_d.rearrange("(k p) n -> p k n", p=128)
            WUQ = MW[:, 0:2304].rearrange("p (k n) -> p k n", n=768)
            WUQR = MW[:, 2304:3072].rearrange("p (k n) -> p k n", n=256)
            WUKV = MW[:, 3072:5120].rearrange("p (k n) -> p k n", n=1024)
            WKR = MW[:, 5120:5376].rearrange("p (k n) -> p k n", n=32)
            wdma(WUQ, wuq_d.rearrange("(k p) n -> p k n", p=128), "wuq")
            wdma(WUKV, wukv_d.rearrange("(k p) n -> p k n", p=128), "wukv")
            wq4 = WUQ.rearrange("p k (h c) -> p k h c", c=96)
            wr4 = WUQR.rearrange("p k (h c) -> p k h c", c=32)
            for k in range(3):
                CP("pool", wr4[:, k, :, 0:16], wq4[:, k, :, 80:96])
                CP("pool", wr4[:, k, :, 16:32], wq4[:, k, :, 64:80])
            WG0 = SL[:, 23:27, :].rearrange("p a b -> p (a b)")[:, 0:8 * 672].rearrange("p (k n) -> p k n", n=672)
            wdma(WG0[:, :, 0:336], w0v[:, :, 0:336], "wst0")
            wdma(WG0[:, :, 336:672], w0v[:, :, 336:672], "wst1")
            for k in range(8):
                CP("pool", WKR[:, k, 0:16], WG0[:, k, 656:672])
                CP("pool", WKR[:, k, 16:32], WG0[:, k, 640:656])
            COS, SIN = sl(15), sl(16)
            CQ = [sl(8 + 3 + i) for i in range(3)]
            CKV = [sl(14), sl(17)]
            KT = sl(18)
            QT = sl(19)
            VA = sl(20).rearrange("p (t c) -> p t c", c=128)
            SZ = sl(21)

            def latent(dsts, col0, gcols, nfeat):
                nch = len(dsts)
                for nb in range(4):
                    ssb = PB[2 + nb % 2]
                    for c in range(nch):
                        bank = nbank()
                        MM(bank[:, :], [(bank[:, :], WG0[:, k, col0 + c * 128:col0 + (c + 1) * 128], SL[:, k, nb * 512:(nb + 1) * 512], k == 0, k == 7) for k in range(8)])
                        ACT(dsts[c][:, nb * 512:(nb + 1) * 512], bank[:, :], AF.Copy)
                        sq = SL[:, 22, (c % 2) * 512:(c % 2 + 1) * 512]
                        ACT(sq, bank[:, :], AF.Square)
                        MM(ssb[:, :], [(ssb[:, :], ones_b, sq, c == 0, c == nch - 1)])
                    r = sc(2)
                    TS("dve", r, ssb[:, :], 1.0 / nfeat, 1e-6, ALU.mult, ALU.add)
                    ACT(r, r, AF.Ln)
                    ACT(r, r, AF.Exp, scale=-0.5)
                    for c in range(nch):
                        STT("dve", dsts[c][:, nb * 512:(nb + 1) * 512], dsts[c][:, nb * 512:(nb + 1) * 512], gcols[:, c:c + 1], r, ALU.mult, ALU.mult)
            latent(CQ, 0, GQ, 384)
            latent(CKV, 384, GKV, 256)
            for nb in range(4):
                ba, bb = nbank(), nbank()
                MM(ba[0:96, :], [(ba[0:96, :], WG0[:, k, 576:672], SL[:, k, nb * 512:(nb + 1) * 512], k == 0, k == 7) for k in range(8)])
                MM(bb[64:96, :], [(bb[64:96, :], WKR[:, k, :], SL[:, k, nb * 512:(nb + 1) * 512], k == 0, k == 7) for k in range(8)])
                t1, t2 = SC[64:96, 2, :], SC[64:96, 3, :]
                TT("dve", t1, ba[64:96, :], COS[64:96, nb * 512:(nb + 1) * 512], ALU.mult)
                TT("dve", t2, bb[64:96, :], SIN[64:96, nb * 512:(nb + 1) * 512], ALU.mult)
                TT("pool", KT[64:96, nb * 512:(nb + 1) * 512], t1, t2, ALU.add)
            sc_mla = 96.0 ** -0.5
            wz_next = [wblock(w0v, 2208, 128, 0)]
            for h in range(8):
                par = h % 2
                ro = (0, 64) if par == 0 else (64, 128)
                rd = (64, 128) if par == 0 else (0, 64)
                if h % 4 == 3:
                    out_proj_load(wout0_d, h // 4)
                if par == 0:
                    wz = wz_next[0]
                    proj_fm(SZ, wz, (0, 128), 128, AF.Silu)
                    if h < 6:
                        wz_next[0] = wblock(w0v, 2208 + (h // 2 + 1) * 128, 128, (h // 2 + 1) % 2)
                for nb in range(4):
                    ba, bb = nbank(), nbank()
                    MM(ba[0:96, :], [(ba[0:96, :], WUQ[:, k, h * 96:(h + 1) * 96], CQ[k][:, nb * 512:(nb + 1) * 512], k == 0, k == 2) for k in range(3)])
                    MM(bb[64:96, :], [(bb[64:96, :], WUQR[:, k, h * 32:(h + 1) * 32], CQ[k][:, nb * 512:(nb + 1) * 512], k == 0, k == 2) for k in range(3)])
                    ACT(QT[0:64, nb * 512:(nb + 1) * 512], ba[0:64, :], AF.Copy)
                    t1, t2 = SC[64:96, 2, :], SC[64:96, 3, :]
                    TT("dve", t1, ba[64:96, :], COS[64:96, nb * 512:(nb + 1) * 512], ALU.mult)
                    TT("dve", t2, bb[64:96, :], SIN[64:96, nb * 512:(nb + 1) * 512], ALU.mult)
                    TT("pool", QT[64:96, nb * 512:(nb + 1) * 512], t1, t2, ALU.add)
                for nb in range(4):
                    bank = nbank()
                    MM(bank[0:64, :], [(bank[0:64, :], WUKV[:, k, h * 128:h * 128 + 64], CKV[k][:, nb * 512:(nb + 1) * 512], k == 0, k == 1) for k in range(2)])
                    ACT(KT[0:64, nb * 512:(nb + 1) * 512], bank[0:64, :], AF.Copy)
                MSET("pool", VA[:, :, rd[0]:rd[1]], 1.0)
                for half in range(2):
                    bank = nbank()
                    mms = []
                    for t8 in range(8):
                        tt = half * 8 + t8
                        for k in range(2):
                            mms.append((bank[:, t8 * 64:(t8 + 1) * 64], CKV[k][:, tt * 128:(tt + 1) * 128], WUKV[:, k, h * 128 + 64:h * 128 + 128], k == 0, k == 1))
                    MM(bank[:, :], mms)
                    CP("dve", VA[:, half * 8:(half + 1) * 8, ro[0]:ro[1]], bank[:, :].rearrange("p (t c) -> p t c", c=64))
                ych = 8 + (h // 2) % 2
                items = [(qb, kt) for qb in range(4) for kt in range(qb * 4 + 4)]
                stt = {}

                def mA(idx, it, h=h):
                    qb, kt = it
                    q0 = qb * 512
                    d = kt - q0 // 128
                    extra = [(d * 128, ident_b, negchk_b)] if d >= 0 else []
                    stt[idx] = sm_A(idx, QT, KT, (0, 96), kt, q0, sc_mla, 0.0, extra)

                def mB(idx, it, ro=ro, rd=rd, ych=ych):
                    qb, kt = it
                    q0 = qb * 512
                    ob = PB[2 + qb % 2]
                    P, c0, n = stt.pop(idx)
                    sm_B(P, c0, n, ob, None, VA[:, kt, :], kt == qb * 4 + 3, kt == 0)
                    if kt == qb * 4 + 3:
                        rc = SC[rd[0]:rd[1], 4, :]
                        RECIP(rc, ob[rd[0]:rd[1], :])
                        tq = SC[ro[0]:ro[1], 5, :]
                        TT("dve", tq, ob[ro[0]:ro[1], :], rc, ALU.mult)
                        TT("pool", sl(ych)[ro[0]:ro[1], q0:q0 + 512], tq, SZ[ro[0]:ro[1], q0:q0 + 512], ALU.mult)
                run_pipeline(items, [(mA, 0), (mB, 3)])
                if h % 4 == 3:
                    out_proj(wout0_d, h // 4, (8, 9))
            if list(layers) == [0, 1]:
                diff_prep(0.8 - 0.6 * math.exp(-0.3 * 1))
            QA, QB, KS, SZ2 = sl(11), sl(16), sl(12), sl(13)
            VS = SL[:, 14:16, :].rearrange("p a b -> p (a b)").rearrange("p (t h c) -> p t h c", h=2, c=128)
            MSET("pool", SL[:, 14:16, :], 0.0)
            MSET("pool", QA, 0.0)
            MSET("pool", QB, 0.0)
            sc_sb = 64.0 ** -0.5
            sbcols = lambda pr: (672 + pr * 128, 1184 + pr * 128, 1696 + pr * 128, 2208 + 512 + pr * 128)
            abuf = [SL[:, 17:19, :].rearrange("p a b -> p (a b)").rearrange("p (k n) -> p k n", n=512),
                    SL[:, 19:21, :].rearrange("p a b -> p (a b)").rearrange("p (k n) -> p k n", n=512)]
            if DEFER:
                mod_dma(1, 0, abuf[0], "ada0")
                mod_dma(1, 1, abuf[1], "ada1")
            wnext = load4(w0v, sbcols(0), 0) if PREFETCH else None
            for pr in range(4):
                wb = wnext if PREFETCH else load4(w0v, sbcols(pr), pr % 2)
                proj_fm2(QA, QB, wb, (0, 128))
                proj_fm(KS, wb, (128, 256), 128)
                proj_fm(SZ2, wb, (384, 512), 128, AF.Silu)
                for q4 in range(4):
                    bank = nbank()
                    mms = []
                    for t4 in range(4):
                        tt = q4 * 4 + t4
                        for k in range(8):
                            mms.append((bank[:, t4 * 128:(t4 + 1) * 128], SL[:, k, tt * 128:(tt + 1) * 128], wb[:, k, 256:384], k == 0, k == 7))
                    MM(bank[:, :], mms)
                    b3 = bank[:, :].rearrange("p (t c) -> p t c", c=128)
                    CP("dve", VS[:, q4 * 4:(q4 + 1) * 4, 0, 0:64], b3[:, :, 0:64])
                    CP("dve", VS[:, q4 * 4:(q4 + 1) * 4, 1, 64:128], b3[:, :, 64:128])
                if pr < 3 and PREFETCH:
                    wnext = load4(w0v, sbcols(pr + 1), (pr + 1) % 2)
                if pr % 2 == 1:
                    out_proj_load(wout0_d, 2 + pr // 2)
                ych = 8 + pr % 2
                items = []
                for qb in range(4):
                    for hh in range(2):
                        nk = qb * 4 + 4
                        for kt in range(nk - 1, -1, -1):
                            items.append((qb, hh, kt, kt == nk - 1, hh == 1 and kt == 0, qb * 2 + hh))
                S21F = SL[:, 21, :].bitcast(F32)
                EB = [sc(0), sc(1), sc(2), S21F[:, 0:512], S21F[:, 512:1024]]

                def geo(it):
                    qb, hh, kt, first, last, chain = it
                    q0 = qb * 512
                    d = kt - q0 // 128
                    c0 = max(d, 0) * 128
                    return q0, d, c0, 512 - c0

                def s_z(idx, it):
                    qb, hh, kt, first, last, chain = it
                    q0, d, c0, n = geo(it)
                    rows = (hh * 64, hh * 64 + 64)
                    if first:
                        MSET("pool", sc(5 + chain % 2), 0.0)
                    zb = PB[idx % 2]
                    mms = [(zb[:, c0:512], KS[:, kt * 128:(kt + 1) * 128], (QA if hh == 0 else QB)[:, q0 + c0:q0 + 512], True, d < 0)]
                    if d >= 0:
                        mms.append((zb[:, c0:c0 + 128], ident_b, negtri_b, False, True))
                    MM(zb[:, c0:512], mms)
                    ACT(EB[idx % 5][:, 0:n], zb[:, c0:512], AF.Exp, scale=sc_sb)

                def s_x(idx, it):
                    q0, d, c0, n = geo(it)
                    csf = sc(3 + idx % 2)
                    ACT(csf[:, 0:n], csf[:, 0:n], AF.Exp, scale=-1.0)
                    W = SL[:, 22, (idx % 2) * 512:(idx % 2) * 512 + 512]
                    TT("pool", W[:, 0:n], EB[idx % 5][:, 0:n], csf[:, 0:n], ALU.mult)

                def s_sp(idx, it):
                    q0, d, c0, n = geo(it)
                    SP = SL[:, 22, (2 + idx % 2) * 512:(2 + idx % 2) * 512 + 512]
                    ACT(SP[:, 0:n], EB[idx % 5][:, 0:n], AF.Ln, bias=1.0)

                def s_cs(idx, it):
                    qb, hh, kt, first, last, chain = it
                    q0, d, c0, n = geo(it)
                    carry = sc(5 + chain % 2)
                    SP = SL[:, 22, (2 + idx % 2) * 512:(2 + idx % 2) * 512 + 512]
                    cb, tb = PB[3 + idx % 2], PB[5 + 2 * (idx % 2)]
                    MM(cb[:, 0:n], [(cb[:, 0:n], U_b, SP[:, 0:n], True, True)])
                    MM(tb[:, 0:n], [(tb[:, 0:n], ones_b, SP[:, 0:n], True, True)])
                    if WARM_DUMMY:
                        MM(PB[6][:, :], [(PB[6][:, :], ones_b, SL[:, 0, 0:512], True, True)] * WARM_DUMMY)
                    csf = sc(3 + idx % 2)
                    TT("dve", csf[:, 0:n], cb[:, 0:n], carry[:, c0:512], ALU.add)
                    TT("dve", carry[:, c0:512], tb[:, 0:n], carry[:, c0:512], ALU.add)

                def s_pv(idx, it, ych=ych):
                    qb, hh, kt, first, last, chain = it
                    q0, d, c0, n = geo(it)
                    W = SL[:, 22, (idx % 2) * 512:(idx % 2) * 512 + 512]
                    ob = PB[2]
                    if first and hh == 0:
                        zinit(ob)
                    MM(ob[:, c0:512], [(ob[:, c0:512], VS[:, kt, hh, :], W[:, 0:n], False, last)])
                    if last:
                        TT("dve", sl(ych)[:, q0:q0 + 512], ob[:, :], SZ2[:, q0:q0 + 512], ALU.mult)
                run_pipeline(items, [(s_z, 0), (s_x, 2), (s_sp, 0), (s_cs, 1), (s_pv, 3)])
                if DEFER and pr < 3:
                    mod_mm(1, 2 * pr, abuf[0])
                    mod_mm(1, 2 * pr + 1, abuf[1])
                    if pr < 2:
                        mod_dma(1, 2 * pr + 2, abuf[0], "ada0")
                        mod_dma(1, 2 * pr + 3, abuf[1], "ada1")
                    else:
                        mod_fin(1)
                if pr % 2 == 1:
                    out_proj(wout0_d, 2 + pr // 2, (8, 9))

        def layer_diff(l, i_layer):
            lam_init = 0.8 - 0.6 * math.exp(-0.3 * i_layer)
            norm_phase(l)
            gate_bc(l)
            diff_prep(lam_init)
            w1v = win1_d.rearrange("(k p) n -> p k n", p=128)
            QD0, QD1, KD, SZ = sl(11), sl(15), sl(12), sl(14)
            VD = sl(13).rearrange("p (t c) -> p t c", c=128)
            MSET("pool", QD0, 0.0)
            MSET("pool", QD1, 0.0)
            sc_d = 64.0 ** -0.5
            dcols = lambda h: (h * 128, 1024 + h * 128, 2048 + h * 128, 3072 + h * 128)
            wnext = load4(w1v, dcols(0), 0) if PREFETCH else None
            for h in range(8):
                wb = wnext if PREFETCH else load4(w1v, dcols(h), h % 2)
                proj_fm2(QD0, QD1, wb, (0, 128))
                proj_fm(KD, wb, (128, 256), 128)
                proj_fm(SZ, wb, (384, 512), 128, AF.Silu)
                for q4 in range(4):
                    bank = nbank()
                    mms = []
                    for t4 in range(4):
                        tt = q4 * 4 + t4
                        for k in range(8):
                            mms.append((bank[:, t4 * 128:(t4 + 1) * 128], SL[:, k, tt * 128:(tt + 1) * 128], wb[:, k, 256:384], k == 0, k == 7))
                    MM(bank[:, :], mms)
                    CP("dve", VD[:, q4 * 4:(q4 + 1) * 4, :], bank[:, :].rearrange("p (t c) -> p t c", c=128))
                if h < 7 and PREFETCH:
                    wnext = load4(w1v, dcols(h + 1), (h + 1) % 2)
                if h % 2 == 1:
                    out_proj_load(wout1_d, h // 2)
                ych = 8 + h % 2
                obs = [PB[2], PB[3]]
                dbs = [PB[4], PB[5]]
                items = [(qb, c, kt) for qb in range(4) for c in range(2) for kt in range(qb * 4 + 4)]
                stt = {}

                def dA(idx, it, h=h):
                    qb, c, kt = it
                    q0 = qb * 512
                    d = kt - q0 // 128
                    extra = []
                    for e4 in range(max(d, 0), 4):
                        dl = d - e4
                        if dl == 0:
                            extra.append((e4 * 128, anti_b, HK[:, h, 0, :]))
                            extra.append((e4 * 128, ident_b, negchk_b))
                        elif dl == -1:
                            extra.append((e4 * 128, anti_b, HK[:, h, 1, :]))
                    stt[idx] = sm_A(idx, QD0 if c == 0 else QD1, KD, (0, 128), kt, q0, sc_d, 0.0, extra, ZB3)

                def dB(idx, it, ych=ych):
                    qb, c, kt = it
                    q0 = qb * 512
                    P, c0, n = stt.pop(idx)
                    last = kt == qb * 4 + 3
                    sm_B(P, c0, n, obs[c], dbs[c], VD[:, kt, :], last, kt == 0)
                    if c == 1 and last:
                        r0, r1, a0, a1 = sc(0), sc(1), sc(2), sc(3)
                        sq = SL[:, 21, 0:512]
                        yv = sl(ych)[:, q0:q0 + 512]
                        szv = SZ[:, q0:q0 + 512]
                        ACT(r0, dbs[0][:, :], AF.Ln)
                        CP("dve", a0, obs[0][:, :])
                        ACT(r1, dbs[1][:, :], AF.Ln)
                        CP("dve", a1, obs[1][:, :])

                        def e1():
                            ACT(r0, r0, AF.Exp, scale=-1.0)
                            ACT(r1, r1, AF.Exp, scale=-1.0)

                        def e2():
                            TT("dve", a0, a0, r0, ALU.mult)
                            TT("dve", a1, a1, r1, ALU.mult)
                            STT("dve", a0, a1, NLAM, a0, ALU.mult, ALU.add)

                        def e3():
                            ACT(sq, a0, AF.Square)
                            MM(PB[6][:, :], [(PB[6][:, :], ones_b, sq, True, True)])

                        def e4():
                            TS("dve", r0, PB[6][:, :], 1.0 / 128, 1e-5, ALU.mult, ALU.add)

                        def e5():
                            ACT(r0, r0, AF.Ln)

                        def e6():
                            ACT(r0, r0, AF.Exp, scale=-0.5)

                        def e7():
                            TT("dve", a0, a0, r0, ALU.mult)
                            STT("dve", yv, a0, SGP, szv, ALU.mult, ALU.mult)
                        defer(1, e1)
                        defer(2, e2)
                        defer(4, e3)
                        defer(5, e4)
                        defer(6, e5)
                        defer(7, e6)
                        defer(8, e7)
                run_pipeline(items, [(dA, 0), (dB, 3)])
                if h % 2 == 1:
                    out_proj(wout1_d, h // 2, (8, 9))

        for l in layers:
            if l % 2 == 0:
                layer_ab(l)
            else:
                layer_diff(l, l)
        if final:
            S.dma("sp", GBC[:], fg_d, "fg")
            SSQ = SM[:, 176:192]
            RSTD = SM[:, 160:176]
            MSET("pool", SSQ, 0.0)
            for tt in range(NT):
                ACT(sc2(0), X[:, tt, :], AF.Square, accum=SSQ[:, tt:tt + 1])
            TS("dve", SSQ, SSQ, 1.0 / D, 1e-6, ALU.mult, ALU.add)
            ACT(SSQ, SSQ, AF.Ln)
            ACT(RSTD, SSQ, AF.Exp, scale=-0.5)
            for tt in range(NT):
                STT("dve", X[:, tt, :], X[:, tt, :], RSTD[:, tt:tt + 1], GBC[:], ALU.mult, ALU.mult)
                S.dma("sp", out_d[tt * 128:(tt + 1) * 128, :], X[:, tt, :], "o%d" % (tt % 8), sb_out=False, sb_in=True)
        else:
            for tt in range(NT):
                S.dma("sp", out_d[tt * 128:(tt + 1) * 128, :], X[:, tt, :], "o%d" % (tt % 8), sb_out=False, sb_in=True)
        S.emit(st)
    return nc


_CACHE = {}
FUSED = True
DEFER_ADA = True
PREFETCH = True
WARM_DUMMY = 0


def _in_maps(inp, xs):
    cst = _consts()
    f = lambda a: np.ascontiguousarray(np.asarray(a, dtype=np.float32))
    maps = []
    for b in range(8):
        m = {
            "x": f(xs[b]),
            "c_t": f(np.asarray(inp["c"])[b].reshape(8, 128).T),
            "posi": np.full((128, 1), int(np.asarray(inp["pos_offset"])[b]), np.int32),
            "ada_w": f(inp["ada_w"]),
            "ada_b_t": f(np.asarray(inp["ada_b"]).reshape(2, 24, 128).transpose(2, 0, 1)),
            "norm_g_t": f(np.asarray(inp["norm_g"]).reshape(2, 8, 128).transpose(2, 0, 1)),
            "final_g_bc": f(np.broadcast_to(np.asarray(inp["final_g"])[None, :], (128, D))),
            "ab_w_in": f(np.asarray(inp["ab_w_in"])[0]),
            "gq_t": f(np.asarray(inp["ab_q_norm_g"])[0].reshape(3, 128).T),
            "gkv_t": f(np.asarray(inp["ab_kv_norm_g"])[0].reshape(2, 128).T),
            "ab_w_uq": f(np.asarray(inp["ab_w_uq"])[0]),
            "ab_w_ukv": f(np.asarray(inp["ab_w_ukv"])[0]),
            "ab_w_out": f(np.asarray(inp["ab_w_out"])[0]),
            "dif_w_in": f(np.asarray(inp["dif_w_in"])[0]),
            "lamv": f(np.broadcast_to(np.stack([np.asarray(inp[k])[0] for k in ("dif_lam_q1", "dif_lam_k1", "dif_lam_q2", "dif_lam_k2")])[None], (128, 4, 64))),
            "subln_t": f(np.asarray(inp["dif_subln_g"])[0].reshape(128, 1)),
            "dif_w_out": f(np.asarray(inp["dif_w_out"])[0]),
            "rel_tab": f(inp["rel_bias_table"]),
        }
        m.update(cst)
        maps.append(m)
    return maps


def _run(layers, final, inp, xs):
    key = (tuple(layers), final)
    if key not in _CACHE:
        _CACHE[key] = build(list(layers), final)
    res = run_bass_kernel_spmd(_CACHE[key], _in_maps(inp, xs), core_ids=list(range(8)))
    return np.stack([r["out"] for r in res.results], 0)


def kernel(**inp):
    x = np.asarray(inp["x"], dtype=np.float32)
    if FUSED:
        return _run((0, 1), True, inp, x)
    x1 = _run((0,), False, inp, x)
    return _run((1,), True, inp, x1)
```
